# Optimizing a Trainium2 kernel written in Bass

```python
import math
import jax
import jax.numpy as jnp
from jax import lax
import numpy as np

D_MODEL = 1024
BATCH = 8
SEQ = 2048
DEPTH = 1

DA_HEADS = 4
DA_HEAD_DIM = 64
DA_V_DIM = 2 * DA_HEAD_DIM
RET_HEADS = 4
RET_QK_DIM = 64
RET_V_DIM = 128
RET_CHUNK = 128
Q_BLOCK = 128
D_FF = 2816
ROPE_THETA = 10000.0
EPS = 1e-6
N_MOD = 9

DA_QK_W = DA_HEADS * 2 * DA_HEAD_DIM
DA_V_W = DA_HEADS * DA_V_DIM
RET_QK_W = RET_HEADS * RET_QK_DIM
RET_V_W = RET_HEADS * RET_V_DIM
IN_SPLITS = (DA_QK_W, DA_QK_W, DA_V_W, RET_QK_W, RET_QK_W, RET_V_W, RET_V_W, D_MODEL, D_MODEL)
IN_WIDTH = 2 * DA_QK_W + DA_V_W + 2 * RET_QK_W + 2 * RET_V_W + 2 * D_MODEL

kernel_name = 'hybrid_diffattn_retention_macaron'


def rmsnorm(x, g):
    xf = x.astype(jnp.float32)
    y = xf * lax.rsqrt(jnp.mean(xf * xf, axis=-1, keepdims=True) + EPS)
    return (y * g.astype(jnp.float32)).astype(x.dtype)


def rope_half(x, pos):
    d = x.shape[-1]
    inv = 1.0 / (ROPE_THETA ** (jnp.arange(0, d, 2, dtype=jnp.float32) / d))
    ang = pos.astype(jnp.float32)[..., None] * inv
    cos = jnp.cos(ang)[:, :, None, :]
    sin = jnp.sin(ang)[:, :, None, :]
    xf = x.astype(jnp.float32)
    x1, x2 = xf[..., : d // 2], xf[..., d // 2:]
    return jnp.concatenate([x1 * cos - x2 * sin, x2 * cos + x1 * sin], axis=-1).astype(x.dtype)


def retnet_rotate(x, pos):
    d = x.shape[-1]
    angle = 1.0 / (ROPE_THETA ** jnp.linspace(0.0, 1.0, d // 2, dtype=jnp.float32))
    ang = pos.astype(jnp.float32)[..., None] * angle
    cos = jnp.cos(ang)[:, :, None, :]
    sin = jnp.sin(ang)[:, :, None, :]
    xf = x.astype(jnp.float32)
    xe, xo = xf[..., 0::2], xf[..., 1::2]
    out = jnp.stack([xe * cos - xo * sin, xo * cos + xe * sin], axis=-1)
    return out.reshape(x.shape).astype(x.dtype)


def swiglu(h, w1, w3, w2):
    return (jax.nn.silu(h @ w1) * (h @ w3)) @ w2


def diff_attention(q, k, v, lam):
    b, nh, _, s, d = q.shape
    nq = s // Q_BLOCK
    scale = d ** -0.5
    qb = q.reshape(b, nh, 2, nq, Q_BLOCK, d).transpose(3, 0, 1, 2, 4, 5)

    def block(qblk):
        sc = jnp.einsum('bhmqd,bhmkd->bhmqk', qblk, k).astype(jnp.float32) * scale
        p = jax.nn.softmax(sc, axis=-1)
        w = p[:, :, 0] - lam * p[:, :, 1]
        return jnp.einsum('bhqk,bhkv->bhqv', w.astype(v.dtype), v)

    o = lax.map(block, qb)
    return o.transpose(1, 0, 3, 2, 4).reshape(b, s, nh, v.shape[-1])


def retention_chunkwise(q, k, v, log_gamma):
    b, nh, s, dk = q.shape
    dv = v.shape[-1]
    n = s // RET_CHUNK
    qc = q.reshape(b, nh, n, RET_CHUNK, dk)
    kc = k.reshape(b, nh, n, RET_CHUNK, dk)
    vc = v.reshape(b, nh, n, RET_CHUNK, dv)
    idx = jnp.arange(RET_CHUNK, dtype=jnp.float32)
    rel = idx[:, None] - idx[None, :]
    lower = rel >= 0
    lg = log_gamma[:, None, None]
    dmat = jnp.where(lower[None], jnp.exp(jnp.where(lower, rel, 0.0)[None] * lg), 0.0)
    scores = jnp.einsum('bhncd,bhnjd->bhncj', qc, kc) * dmat[None, :, None]
    intra = jnp.einsum('bhncj,bhnje->bhnce', scores, vc)
    zeta = jnp.exp((RET_CHUNK - 1.0 - idx)[None, :] * log_gamma[:, None])
    xi = jnp.exp((idx + 1.0)[None, :] * log_gamma[:, None])
    kv = jnp.einsum('bhnjd,bhnje->nbhde', kc * zeta[None, :, None, :, None], vc)
    chunk_decay = jnp.exp(RET_CHUNK * log_gamma)[None, :, None, None]

    def step(state, kv_n):
        return state * chunk_decay + kv_n, state

    _, r_prev = lax.scan(step, jnp.zeros((b, nh, dk, dv), jnp.float32), kv)
    cross = jnp.einsum('bhncd,nbhde->bhnce', qc * xi[None, :, None, :, None], r_prev)
    return (intra + cross).reshape(b, nh, s, dv)


def token_mixing(h, positions, w_in, b_merge, da_q_gain, da_k_gain, lq1, lk1, lq2, lk2,
                 da_subln, ret_decay_f, ret_decay_b, ret_norm, w_branch_a, w_branch_r,
                 w_out, lam_init):
    b, s, _ = h.shape
    z = h @ w_in
    split_idx = [int(i) for i in np.cumsum(IN_SPLITS)[:-1]]
    qa, ka, va, qr, kr, vr, ret_gate, merge_a, merge_r = jnp.split(z, split_idx, axis=-1)

    qa = rope_half(rmsnorm(qa.reshape(b, s, DA_HEADS * 2, DA_HEAD_DIM), da_q_gain), positions)
    ka = rope_half(rmsnorm(ka.reshape(b, s, DA_HEADS * 2, DA_HEAD_DIM), da_k_gain), positions)
    qa = qa.reshape(b, s, DA_HEADS, 2, DA_HEAD_DIM).transpose(0, 2, 3, 1, 4)
    ka = ka.reshape(b, s, DA_HEADS, 2, DA_HEAD_DIM).transpose(0, 2, 3, 1, 4)
    va = va.reshape(b, s, DA_HEADS, DA_V_DIM).transpose(0, 2, 1, 3)
    f32 = jnp.float32
    lam = (jnp.exp(jnp.sum(lq1.astype(f32) * lk1.astype(f32)))
           - jnp.exp(jnp.sum(lq2.astype(f32) * lk2.astype(f32))) + lam_init)
    oa = diff_attention(qa, ka, va, lam)
    oa = (rmsnorm(oa, da_subln) * (1.0 - lam_init)).reshape(b, s, DA_V_W)

    qr = retnet_rotate(qr.reshape(b, s, RET_HEADS, RET_QK_DIM), positions)
    kr = retnet_rotate(kr.reshape(b, s, RET_HEADS, RET_QK_DIM), positions) * (RET_QK_DIM ** -0.5)
    qr = qr.astype(f32).transpose(0, 2, 1, 3)
    kr = kr.astype(f32).transpose(0, 2, 1, 3)
    vr = vr.reshape(b, s, RET_HEADS, RET_V_DIM).astype(f32).transpose(0, 2, 1, 3)
    lg_f = jax.nn.log_sigmoid(ret_decay_f.astype(f32))
    lg_b = jax.nn.log_sigmoid(ret_decay_b.astype(f32))
    y_f = retention_chunkwise(qr, kr, vr, lg_f)
    y_b = jnp.flip(retention_chunkwise(jnp.flip(qr, 2), jnp.flip(kr, 2), jnp.flip(vr, 2), lg_b), 2)
    y = (y_f + y_b).transpose(0, 2, 1, 3).astype(h.dtype)
    y = rmsnorm(y, ret_norm).reshape(b, s, RET_V_W) * jax.nn.silu(ret_gate)

    p_a = oa @ w_branch_a
    p_r = y @ w_branch_r
    merged = (jax.nn.sigmoid(merge_a + b_merge[0]) * p_a
              + jax.nn.sigmoid(merge_r + b_merge[1]) * p_r)
    return merged @ w_out


def setup_inputs(seed: int = 0) -> dict:
    key = jax.random.key(seed)
    ks = jax.random.split(key, 32)
    f32 = jnp.float32
    L = DEPTH
    D = D_MODEL

    def nrm(k, shape, scale):
        return jax.random.normal(k, shape, f32) * scale

    def gain(k, shape):
        return 1.0 + 0.05 * jax.random.normal(k, shape, f32)

    base_logit = jnp.log(2.0 ** (5.0 + jnp.arange(RET_HEADS, dtype=f32)) - 1.0)
    offset = jax.random.randint(ks[2], (BATCH, 1), 0, 1024, dtype=jnp.int32)
    positions = jnp.arange(SEQ, dtype=jnp.int32)[None, :] + offset
    return {
        'x': nrm(ks[0], (BATCH, SEQ, D), 1.0),
        'c': nrm(ks[1], (BATCH, D), 1.0),
        'positions': positions,
        'w_ada': nrm(ks[3], (L, D, N_MOD * D), D ** -0.5),
        'b_ada': nrm(ks[4], (L, N_MOD * D), 0.02),
        'norm_ffn1': gain(ks[5], (L, D)),
        'ffn1_w1': nrm(ks[6], (L, D, D_FF), D ** -0.5),
        'ffn1_w3': nrm(ks[7], (L, D, D_FF), D ** -0.5),
        'ffn1_w2': nrm(ks[8], (L, D_FF, D), D_FF ** -0.5),
        'norm_mix': gain(ks[9], (L, D)),
        'w_in': nrm(ks[10], (L, D, IN_WIDTH), D ** -0.5),
        'b_merge': nrm(ks[11], (L, 2, D), 0.02),
        'da_q_gain': gain(ks[12], (L, DA_HEAD_DIM)),
        'da_k_gain': gain(ks[13], (L, DA_HEAD_DIM)),
        'da_lambda_q1': nrm(ks[14], (L, DA_HEAD_DIM), 0.1),
        'da_lambda_k1': nrm(ks[15], (L, DA_HEAD_DIM), 0.1),
        'da_lambda_q2': nrm(ks[16], (L, DA_HEAD_DIM), 0.1),
        'da_lambda_k2': nrm(ks[17], (L, DA_HEAD_DIM), 0.1),
        'da_subln': gain(ks[18], (L, DA_V_DIM)),
        'ret_decay_f': base_logit[None] + nrm(ks[19], (L, RET_HEADS), 0.1),
        'ret_decay_b': base_logit[None] + nrm(ks[20], (L, RET_HEADS), 0.1),
        'ret_norm': gain(ks[21], (L, RET_V_DIM)),
        'w_branch_a': nrm(ks[22], (L, DA_V_W, D), DA_V_W ** -0.5),
        'w_branch_r': nrm(ks[23], (L, RET_V_W, D), RET_V_W ** -0.5),
        'w_out': nrm(ks[24], (L, D, D), D ** -0.5),
        'norm_ffn2': gain(ks[25], (L, D)),
        'ffn2_w1': nrm(ks[26], (L, D, D_FF), D ** -0.5),
        'ffn2_w3': nrm(ks[27], (L, D, D_FF), D ** -0.5),
        'ffn2_w2': nrm(ks[28], (L, D_FF, D), D_FF ** -0.5),
    }


def reference(x, c, positions, w_ada, b_ada, norm_ffn1, ffn1_w1, ffn1_w3, ffn1_w2,
              norm_mix, w_in, b_merge, da_q_gain, da_k_gain, da_lambda_q1, da_lambda_k1,
              da_lambda_q2, da_lambda_k2, da_subln, ret_decay_f, ret_decay_b, ret_norm,
              w_branch_a, w_branch_r, w_out, norm_ffn2, ffn2_w1, ffn2_w3, ffn2_w2):
    b = x.shape[0]
    for l in range(DEPTH):
        lam_init = 0.8 - 0.6 * math.exp(-0.3 * l)
        mod = (jax.nn.silu(c) @ w_ada[l] + b_ada[l]).reshape(b, N_MOD, 1, D_MODEL)
        sh1, sc1, g1, sh2, sc2, g2, sh3, sc3, g3 = [mod[:, i] for i in range(N_MOD)]

        h = rmsnorm(x, norm_ffn1[l]) * (1.0 + sc1) + sh1
        x = x + 0.5 * g1 * swiglu(h, ffn1_w1[l], ffn1_w3[l], ffn1_w2[l])

        h = rmsnorm(x, norm_mix[l]) * (1.0 + sc2) + sh2
        x = x + g2 * token_mixing(h, positions, w_in[l], b_merge[l], da_q_gain[l], da_k_gain[l],
                                  da_lambda_q1[l], da_lambda_k1[l], da_lambda_q2[l], da_lambda_k2[l],
                                  da_subln[l], ret_decay_f[l], ret_decay_b[l], ret_norm[l],
                                  w_branch_a[l], w_branch_r[l], w_out[l], lam_init)

        h = rmsnorm(x, norm_ffn2[l]) * (1.0 + sc3) + sh3
        x = x + 0.5 * g3 * swiglu(h, ffn2_w1[l], ffn2_w3[l], ffn2_w2[l])
    return x
```

```python
import math
import os
import numpy as np
import concourse.bass as bass
import concourse.mybir as mybir
from concourse.bass_utils import run_bass_kernel_spmd

F32 = mybir.dt.float32
BF16 = mybir.dt.bfloat16
I32 = mybir.dt.int32
AF = mybir.ActivationFunctionType
ALU = mybir.AluOpType
AX = mybir.AxisListType

D = 1024
S = 2048
NB = 8
DFF = 2816
NCH = D // 128
NFC = DFF // 128
NT = S // 128
EPS = 1e-6
INW = 5120
LAM_INIT = 0.8 - 0.6 * math.exp(-0.3 * 0)

STAGE = int(os.environ.get("MK_STAGE", "9"))


def _dsize(dt):
    return {F32: 4, BF16: 2, I32: 4}[dt]


class Prog:
    ENG = ('pe', 'act', 'dve', 'pool', 'sp')

    def __init__(self, nc):
        self.nc = nc
        self.q = {e: [] for e in self.ENG}
        self.sems = {}
        self.cnt = {}
        self.seen = {e: {} for e in self.ENG}
        self.bufs = {}
        self.regions = {}
        self.region_init = {}
        for e in ('pe', 'act', 'dve', 'pool'):
            self._sem('E_' + e)

    def _sem(self, name):
        if name not in self.sems:
            self.sems[name] = self.nc.alloc_semaphore(name)
            self.cnt[name] = 0
        return name

    def region(self, name, space, off, nbytes):
        assert name not in self.regions, name
        merged = {}
        for n2, (sp2, o2, b2) in self.regions.items():
            if sp2 == space and o2 < off + nbytes and off < o2 + b2:
                for kk, b in self.bufs.items():
                    nm = kk[0] if isinstance(kk, tuple) else kk
                    if nm == n2:
                        if b[0] is not None:
                            merged[b[0][0]] = max(merged.get(b[0][0], 0), b[0][1])
                        for s, v in b[1].items():
                            merged[s] = max(merged.get(s, 0), v)
                for s, v in self.region_init.get(n2, {}).items():
                    merged[s] = max(merged.get(s, 0), v)
        self.regions[name] = (space, off, nbytes)
        self.region_init[name] = merged

    def _get(self, k):
        b = self.bufs.get(k)
        if b is None:
            nm = k[0] if isinstance(k, tuple) else k
            init = self.region_init.get(nm)
            if init:
                return [None, dict(init)]
            return None
        return b

    def _need(self, e, tok, waits):
        if tok is None:
            return
        s, v = tok
        if e == 'pe' and s == 'E_pe':
            return
        if self.seen[e].get(s, 0) >= v:
            return
        waits[s] = max(waits.get(s, 0), v)

    def _deps(self, e, reads, writes):
        waits = {}
        for k in reads:
            b = self._get(k)
            if b:
                self._need(e, b[0], waits)
        for k in writes:
            b = self._get(k)
            if b:
                self._need(e, b[0], waits)
                for s, v in b[1].items():
                    self._need(e, (s, v), waits)
        for s, v in waits.items():
            self.seen[e][s] = v
            h = self.sems[s]
            self.q[e].append(lambda eng, h=h, v=v: eng.wait_ge(h, v))

    def _commit(self, tok, reads, writes):
        s, v = tok
        for k in reads:
            b = self.bufs.get(k)
            if b is None:
                g = self._get(k)
                b = self.bufs[k] = [None, dict(g[1])] if g else [None, {}]
            b[1][s] = max(b[1].get(s, 0), v)
        for k in writes:
            self.bufs[k] = [tok, {}]

    def op(self, e, fn, reads=(), writes=(), mark=True):
        self._deps(e, reads, writes)
        s = 'E_' + e
        if mark:
            self.cnt[s] += 1
            h = self.sems[s]
            self.q[e].append(lambda eng, fn=fn, h=h: fn(eng).then_inc(h, 1))
            tok = (s, self.cnt[s])
        else:
            self.q[e].append(lambda eng, fn=fn: fn(eng))
            tok = (s, self.cnt[s] + 1)
        self._commit(tok, reads, writes)

    def dma(self, qe, pieces, reads=(), writes=(), sem=None):
        self._deps(qe, reads, writes)
        if sem is None:
            k = (writes[0] if writes else reads[0])
            sem = 'D_' + str(k)
        s = self._sem(sem)
        h = self.sems[s]
        for (o, i) in pieces:
            self.cnt[s] += 16
            self.q[qe].append(lambda eng, o=o, i=i, h=h: eng.dma_start(out=o, in_=i).then_inc(h, 16))
        tok = (s, self.cnt[s])
        self._commit(tok, reads, writes)

    def finish(self):
        for e in self.ENG:
            for s, c in self.cnt.items():
                if c > 0 and self.seen[e].get(s, 0) < c and s != 'E_' + e:
                    h = self.sems[s]
                    self.q[e].append(lambda eng, h=h, c=c: eng.wait_ge(h, c))
                    self.seen[e][s] = c

    def emit(self):
        nc = self.nc
        with nc.Block() as block:
            @block.tensor
            def _(eng):
                for f in self.q['pe']:
                    f(eng)

            @block.scalar
            def _(eng):
                for f in self.q['act']:
                    f(eng)

            @block.vector
            def _(eng):
                for f in self.q['dve']:
                    f(eng)

            @block.gpsimd
            def _(eng):
                for f in self.q['pool']:
                    f(eng)

            @block.sync
            def _(eng):
                for f in self.q['sp']:
                    f(eng)


class Mem:
    def __init__(self, p, nc, nbytes):
        self.p = p
        self.nbytes = nbytes
        self.t = nc.alloc_sbuf_tensor('arena', [128, nbytes // 4], F32)

    def view(self, name, off, dtype, shape, parts=128):
        n = 1
        for s in shape:
            n *= s
        nb = n * _dsize(dtype)
        assert off % 4 == 0 and off + nb <= self.nbytes, (name, off, nb, self.nbytes)
        self.p.region(name, 'sb', off, nb)
        es = _dsize(dtype)
        ap = self.t.bitcast(dtype)[0:parts, off // es: off // es + n]
        if len(shape) == 2:
            ap = ap.rearrange("p (a b) -> p a b", a=shape[0])
        elif len(shape) == 3:
            ap = ap.rearrange("p (a b c) -> p a b c", a=shape[0], b=shape[1])
        elif len(shape) == 4:
            ap = ap.rearrange("p (a b c d) -> p a b c d", a=shape[0], b=shape[1], c=shape[2])
        return ap


def build_program():
    nc = bass.Bass("TRN2", target_bir_lowering=False)
    p = Prog(nc)

    def din(name, shape, dt=F32):
        return nc.dram_tensor(name, list(shape), dt, kind="ExternalInput").ap()

    x_d = din("x", [S, D])
    out_d = nc.dram_tensor("out", [S, D], F32, kind="ExternalOutput").ap()
    cT_d = din("cT", [128, NCH])
    pos_d = din("pos", [128, S], I32)
    wada_d = din("w_ada", [D, 9 * D])
    bada_d = din("b_adaT", [128, 72])
    gains_d = din("gainsT", [128, 3 * NCH])
    f1w1_d = din("f1w1", [D, DFF]); f1w3_d = din("f1w3", [D, DFF]); f1w2_d = din("f1w2", [DFF, D])
    f2w1_d = din("f2w1", [D, DFF]); f2w3_d = din("f2w3", [D, DFF]); f2w2_d = din("f2w2", [DFF, D])
    win_d = din("w_in", [D, INW])
    ident_d = din("ident", [128, 128])
    ones_d = din("ones", [128, 128])
    blk_d = din("blk", [128, 128]); rmat_d = din("rmat", [128, 128])
    inv_d = din("inv", [128, 2]); sgn_d = din("sgn", [128, 1])
    etab_d = din("etab", [128, 640]); diag2_d = din("diag2", [128, 640])
    gqk_d = din("gqk", [128, 2]); lqk_d = din("lqk", [128, 256])
    gsub_d = din("gsub", [128, 128]); gret_d = din("gret", [128, 128])
    dec_d = din("dec", [128, 8]); bm_d = din("bmT", [128, 16])
    wba_d = din("w_ba", [512, D]); wbr_d = din("w_br", [512, D]); wout_d = din("w_out", [D, D])

    mem = Mem(p, nc, 206 * 1024)
    pp = [nc.alloc_psum_tensor('pp%d' % i, [128, 1024], F32) for i in range(4)]

    class PB:
        def __init__(self, t, base, width):
            self.t, self.base, self.width = t, base, width

        def __getitem__(self, idx):
            rows, cols = idx
            a = cols.start or 0
            b = self.width if cols.stop is None else cols.stop
            assert cols.step is None and 0 <= a < b <= self.width
            return self.t[rows, self.base + a:self.base + b]

        def bitcast(self, dt):
            assert dt == BF16
            return PB(self.t.bitcast(dt), self.base * 2, self.width * 2)

    ps = [PB(pp[i // 2], (i % 2) * 512, 512) for i in range(8)]
    for i in range(8):
        p.region('ps%d' % i, 'ps', i, 1)

    OFF_XT = 0
    OFF_CONST = 64 * 1024
    OFF_FREE = 70 * 1024
    XT = mem.view('XT', OFF_XT, F32, [NCH, S])
    co = OFF_CONST
    IDF = mem.view('IDF', co, F32, [128]); co += 512
    ONESF = mem.view('ONESF', co, F32, [128]); co += 512
    IDB = mem.view('IDB', co, BF16, [128]); co += 256
    MOD = mem.view('MOD', co, F32, [72]); co += 288
    BADA = mem.view('BADA', co, F32, [72]); co += 288
    GAINS = mem.view('GAINS', co, F32, [24]); co += 96
    CT = mem.view('CT', co, F32, [NCH]); co += 32
    SCB = mem.view('SCB', co, BF16, [NCH]); co += 16 + 16
    NA = mem.view('NA', co, F32, [24]); co += 96
    HG = mem.view('HG', co, F32, [24]); co += 96
    EPSC = mem.view('EPSC', co, F32, [1]); co += 4
    BLKF = mem.view('BLKF', co, F32, [128]); co += 512
    RMB = mem.view('RMB', co, BF16, [128]); co += 256
    INV = mem.view('INV', co, F32, [2]); co += 8
    SGN = mem.view('SGN', co, F32, [1]); co += 4
    GQK = mem.view('GQK', co, F32, [2]); co += 8
    GSUB = mem.view('GSUB', co, F32, [128]); co += 512
    GRET = mem.view('GRET', co, F32, [128]); co += 512
    DEC = mem.view('DEC', co, F32, [8]); co += 32
    BM = mem.view('BM', co, F32, [16]); co += 64
    LGN = mem.view('LGN', co, F32, [8]); co += 32
    LG = mem.view('LG', co, F32, [8]); co += 32
    SCT = mem.view('SCT', co, F32, [8, 14]); co += 448
    NLAM = mem.view('NLAM', co, F32, [1]); co += 4
    LTMP = mem.view('LTMP', co, F32, [8]); co += 32
    assert co <= OFF_FREE, co

    p.dma('sp', [(IDF, ident_d), (ONESF, ones_d), (BADA, bada_d), (GAINS, gains_d), (CT, cT_d)],
          writes=['IDF', 'ONESF', 'BADA', 'GAINS', 'CT'], sem='D_const')
    p.dma('pool', [(IDB, ident_d), (RMB, rmat_d)], writes=['IDB', 'RMB'], sem='D_constb')
    p.dma('sp', [(BLKF, blk_d), (INV, inv_d), (SGN, sgn_d), (GQK, gqk_d), (GSUB, gsub_d), (GRET, gret_d),
                 (DEC, dec_d), (BM, bm_d)],
          writes=['BLKF', 'INV', 'SGN', 'GQK', 'GSUB', 'GRET', 'DEC', 'BM'], sem='D_const2')
    p.op('dve', lambda e: e.memset(EPSC, EPS), writes=['EPSC'])

    p.op('act', lambda e: e.activation(out=SCB, in_=CT, func=AF.Silu), reads=['CT'], writes=['SCB'])
    WA_OFF = OFF_FREE + 40 * 1024
    wa = [mem.view('WA%d' % i, WA_OFF + i * 16384, BF16, [NCH, 1024]) for i in range(2)]
    wada_v = wada_d.rearrange("(c p) n -> p c n", p=128)
    for g in range(9):
        sl = g % 2
        key = 'WA%d' % sl
        p.dma('pool', [(wa[sl][:, c4 * 2:(c4 + 1) * 2, :], wada_v[:, c4 * 2:(c4 + 1) * 2, g * 1024:(g + 1) * 1024])
                       for c4 in range(4)], writes=[key])
        for j in range(8):
            col = g * 8 + j
            for k in range(NCH):
                p.op('pe', lambda e, sl=sl, j=j, k=k, col=col: e.matmul(
                    ps[7][:, col:col + 1], lhsT=wa[sl][:, k, j * 128:(j + 1) * 128], rhs=SCB[:, k:k + 1],
                    start=(k == 0), stop=(k == NCH - 1)),
                    reads=[key, 'SCB'], writes=['ps7'], mark=(k == NCH - 1 and j == 7))
    p.op('dve', lambda e: e.tensor_tensor(out=MOD, in0=ps[7][:, 0:72], in1=BADA, op=ALU.add),
         reads=['ps7', 'BADA'], writes=['MOD'])
    for n in range(3):
        sc = MOD[:, (3 * n + 1) * 8:(3 * n + 2) * 8]
        p.op('dve', lambda e, n=n, sc=sc: e.scalar_tensor_tensor(
            out=NA[:, n * 8:(n + 1) * 8], in0=sc, scalar=1.0, in1=GAINS[:, n * 8:(n + 1) * 8],
            op0=ALU.add, op1=ALU.mult), reads=['MOD', 'GAINS'], writes=[('NA', n)])
        gt = MOD[:, (3 * n + 2) * 8:(3 * n + 3) * 8]
        p.op('dve', lambda e, n=n, gt=gt: e.tensor_scalar(
            out=HG[:, n * 8:(n + 1) * 8], in0=gt, scalar1=(1.0 if n == 1 else 0.5), scalar2=None,
            op0=ALU.mult), reads=['MOD'], writes=[('HG', n)])

    XS_OFF = OFF_FREE
    xs = [mem.view('XS%d' % i, XS_OFF + i * 16384, F32, [4, D]) for i in range(2)]
    x_v = x_d.rearrange("(t p) d -> p t d", p=128)
    for g in range(4):
        sl = g % 2
        key = 'XS%d' % sl
        p.dma('sp', [(xs[sl][:, t:t + 1, :], x_v[:, g * 4 + t:g * 4 + t + 1, :]) for t in range(4)], writes=[key])
        for c in range(NCH):
            pb = (g * NCH + c) % 2
            pk = 'ps%d' % pb
            for t in range(4):
                p.op('pe', lambda e, sl=sl, t=t, c=c, pb=pb: e.transpose(
                    ps[pb][:, t * 128:(t + 1) * 128], xs[sl][:, t, c * 128:(c + 1) * 128], IDF),
                    reads=[key, 'IDF'], writes=[pk], mark=(t == 3))
            eng = 'dve' if c % 2 == 0 else 'act'
            dst = XT[:, c, g * 512:(g + 1) * 512]
            if eng == 'dve':
                p.op('dve', lambda e, pb=pb, dst=dst: e.tensor_copy(out=dst, in_=ps[pb][:, :]),
                     reads=[pk], writes=[('XT', c, g)])
            else:
                p.op('act', lambda e, pb=pb, dst=dst: e.copy(out=dst, in_=ps[pb][:, :]),
                     reads=[pk], writes=[('XT', c, g)])


    O_HTG = OFF_FREE
    O_GT = OFF_FREE + 16 * 1024
    O_W13 = OFF_FREE + 60 * 1024
    O_W2 = OFF_FREE + 84 * 1024
    O_SCR = OFF_FREE + 106 * 1024

    def norm_scratch(tagp, ntok, scr_off):
        SQ = mem.view(tagp + 'SQ', scr_off, F32, [ntok])
        RS = mem.view(tagp + 'RS', scr_off + 4 * ntok, F32, [ntok])
        TM = [mem.view(tagp + 'TM%d' % i, scr_off + 8 * ntok + i * 4 * ntok, F32, [ntok]) for i in range(2)]
        return SQ, RS, TM

    def rms_norm_tokens(n, tok0, ntok, HT_dst, tagp, scr, hkeyf, add_eng='dve'):
        SQ, RS, TM = scr
        g0 = tok0 // 512
        ng = ntok // 512
        for c in range(NCH):
            xk = [('XT', c, g0 + i) for i in range(ng)]
            src = XT[:, c, tok0:tok0 + ntok]
            if c == 0:
                p.op('act', lambda e, src=src: e.activation(out=SQ, in_=src, func=AF.Square),
                     reads=xk, writes=[tagp + 'SQ'])
            else:
                t = TM[c % 2]
                tk = tagp + 'TM%d' % (c % 2)
                p.op('act', lambda e, src=src, t=t: e.activation(out=t, in_=src, func=AF.Square),
                     reads=xk, writes=[tk])
                p.op(add_eng, lambda e, t=t: e.tensor_tensor(out=SQ, in0=SQ, in1=t, op=ALU.add),
                     reads=[tk, tagp + 'SQ'], writes=[tagp + 'SQ'])
        for i in range(ng):
            pb = 6 + (i % 2)
            pk = 'ps%d' % pb
            p.op('pe', lambda e, i=i, pb=pb: e.matmul(ps[pb][:, :], lhsT=ONESF, rhs=SQ[:, i * 512:(i + 1) * 512],
                                                     start=True, stop=True),
                 reads=[tagp + 'SQ', 'ONESF'], writes=[pk])
            t = TM[i % 2]
            tk = tagp + 'TM%d' % (i % 2)
            p.op('act', lambda e, pb=pb, t=t: e.activation(out=t[:, 0:512], in_=ps[pb][:, :], func=AF.Ln,
                                                           scale=1.0 / D, bias=EPSC),
                 reads=[pk, 'EPSC'], writes=[tk])
            p.op('act', lambda e, i=i, t=t: e.activation(out=RS[:, i * 512:(i + 1) * 512], in_=t[:, 0:512],
                                                         func=AF.Exp, scale=-0.5),
                 reads=[tk], writes=[(tagp + 'RS', i)])
        for c in range(NCH):
            xk = [('XT', c, g0 + i) for i in range(ng)]
            t = TM[c % 2]
            tk = tagp + 'TM%d' % (c % 2)
            p.op('dve', lambda e, c=c, t=t: e.tensor_tensor(out=t, in0=XT[:, c, tok0:tok0 + ntok], in1=RS, op=ALU.mult),
                 reads=xk + [(tagp + 'RS', i) for i in range(ng)], writes=[tk])
            p.op('act', lambda e, c=c, t=t: e.activation(out=HT_dst[:, c, 0:ntok], in_=t, func=AF.Identity,
                                                         scale=NA[:, n * 8 + c:n * 8 + c + 1],
                                                         bias=MOD[:, 3 * n * 8 + c:3 * n * 8 + c + 1]),
                 reads=[tk, ('NA', n), 'MOD'], writes=[hkeyf(c)])

    HT_dst_key = ['HTG']

    def ffn(n, w1_d, w3_d, w2_d):
        TG = 1024
        U = 'f%d_' % n
        scr = norm_scratch(U, TG, O_SCR)
        HTG = mem.view(U + 'HTG', O_HTG, BF16, [NCH, TG])
        GT = mem.view(U + 'GT', O_GT, BF16, [NFC, TG])
        W1s = [mem.view(U + 'W1s%d' % i, O_W13 + i * 8192, BF16, [NCH, 256]) for i in range(3)]
        W3s = [mem.view(U + 'W3s%d' % i, O_W13 + i * 8192 + 4096, BF16, [NCH, 256]) for i in range(3)]
        W2s = [mem.view(U + 'W2s%d' % i, O_W2 + i * 11264, BF16, [NFC, 256]) for i in range(2)]
        SL = [mem.view(U + 'SL%d' % i, O_SCR + 16384 + i * 2048, F32, [512]) for i in range(2)]
        w1_v = w1_d.rearrange("(c p) n -> p c n", p=128)
        w3_v = w3_d.rearrange("(c p) n -> p c n", p=128)
        w2_v = w2_d.rearrange("(f p) n -> p f n", p=128)
        loads = []
        for tg in range(2):
            for j in range(11):
                loads.append(('a', j))
            for j in range(4):
                loads.append(('b', j))
        state = {'nxt': 0}

        def prefetch(i):
            while state['nxt'] < len(loads):
                kind, j = loads[state['nxt']]
                if state['nxt'] > i + (2 if kind == 'a' else 1):
                    break
                if kind == 'a':
                    sl = j % 3
                    p.dma('pool', [(W1s[sl][:, 0:4, :], w1_v[:, 0:4, j * 256:(j + 1) * 256]),
                                   (W1s[sl][:, 4:8, :], w1_v[:, 4:8, j * 256:(j + 1) * 256]),
                                   (W3s[sl][:, 0:4, :], w3_v[:, 0:4, j * 256:(j + 1) * 256]),
                                   (W3s[sl][:, 4:8, :], w3_v[:, 4:8, j * 256:(j + 1) * 256])],
                          writes=[U + 'W1s%d' % sl, U + 'W3s%d' % sl], sem='D_W13_%d' % sl)
                else:
                    sl = j % 2
                    p.dma('pool', [(W2s[sl][:, a:b, :], w2_v[:, a:b, j * 256:(j + 1) * 256])
                                   for (a, b) in ((0, 6), (6, 11), (11, 17), (17, 22))],
                          writes=[U + 'W2s%d' % sl], sem='D_W2_%d' % sl)
                state['nxt'] += 1

        li = 0
        for tg in range(2):
            tok0 = tg * TG
            prefetch(li)
            HT_dst_key[0] = U + 'HTG'
            rms_norm_tokens(n, tok0, TG, HTG, U, scr, lambda c: (U + 'HTG', c))
            for j in range(11):
                prefetch(li)
                sl = j % 3
                for fl in range(2):
                    f = j * 2 + fl
                    for half in range(2):
                        q = (f * 2 + half) % 2
                        A, Bk = ps[2 * q], ps[2 * q + 1]
                        ak, bk = 'ps%d' % (2 * q), 'ps%d' % (2 * q + 1)
                        for k in range(NCH):
                            p.op('pe', lambda e, A=A, sl=sl, k=k, fl=fl, half=half: e.matmul(
                                A[:, :], lhsT=W1s[sl][:, k, fl * 128:(fl + 1) * 128],
                                rhs=HTG[:, k, half * 512:(half + 1) * 512], start=(k == 0), stop=(k == NCH - 1)),
                                reads=[U + 'W1s%d' % sl, (U + 'HTG', k)], writes=[ak], mark=(k == NCH - 1))
                        for k in range(NCH):
                            p.op('pe', lambda e, Bk=Bk, sl=sl, k=k, fl=fl, half=half: e.matmul(
                                Bk[:, :], lhsT=W3s[sl][:, k, fl * 128:(fl + 1) * 128],
                                rhs=HTG[:, k, half * 512:(half + 1) * 512], start=(k == 0), stop=(k == NCH - 1)),
                                reads=[U + 'W3s%d' % sl, (U + 'HTG', k)], writes=[bk], mark=(k == NCH - 1))
                        p.op('act', lambda e, A=A, q=q: e.activation(out=SL[q], in_=A[:, :], func=AF.Silu),
                             reads=[ak], writes=[U + 'SL%d' % q])
                        p.op('dve', lambda e, Bk=Bk, q=q, f=f, half=half: e.tensor_tensor(
                            out=GT[:, f, half * 512:(half + 1) * 512], in0=SL[q], in1=Bk[:, :], op=ALU.mult),
                            reads=[U + 'SL%d' % q, bk], writes=[(U + 'GT', f, half)])
                li += 1
            for j in range(4):
                prefetch(li)
                sl = j % 2
                for dl in range(2):
                    d = j * 2 + dl
                    for half in range(2):
                        pb = 4 + (d * 2 + half) % 2
                        ck = 'ps%d' % pb
                        for f in range(NFC):
                            p.op('pe', lambda e, pb=pb, sl=sl, f=f, dl=dl, half=half: e.matmul(
                                ps[pb][:, :], lhsT=W2s[sl][:, f, dl * 128:(dl + 1) * 128],
                                rhs=GT[:, f, half * 512:(half + 1) * 512], start=(f == 0), stop=(f == NFC - 1)),
                                reads=[U + 'W2s%d' % sl, (U + 'GT', f, half)], writes=[ck], mark=(f == NFC - 1))
                        g = tg * 2 + half
                        xs_ = XT[:, d, g * 512:(g + 1) * 512]
                        p.op('dve', lambda e, pb=pb, d=d, xs_=xs_: e.scalar_tensor_tensor(
                            out=xs_, in0=ps[pb][:, :], scalar=HG[:, n * 8 + d:n * 8 + d + 1], in1=xs_,
                            op0=ALU.mult, op1=ALU.add),
                            reads=[ck, ('HG', n), ('XT', d, g)], writes=[('XT', d, g)])
                li += 1

    if STAGE >= 1:
        ffn(0, f1w1_d, f1w3_d, f1w2_d)


    PI_LO = 3.1415925
    TWO_PI = 2.0 * math.pi
    CW1 = 6.28125
    CW2 = TWO_PI - 6.28125

    def mixer():
        n = 1
        M0 = OFF_FREE
        O_HT = M0
        O_RT = M0 + 32 * 1024
        O_RQ = M0 + 48 * 1024
        O_OA = O_RQ + 25088
        O_Y = O_OA + 16384
        O_W = O_Y + 16384
        O_SC = O_W + 12288
        HT = mem.view('HT', O_HT, BF16, [NCH, S])
        OAT = mem.view('OAT', O_OA, BF16, [4, S])
        WS = [mem.view('WS%d' % i, O_W + i * 4096, BF16, [NCH, 256]) for i in range(3)]
        win_v = win_d.rearrange("(c p) n -> p c n", p=128)
        wst = {'i': 0}

        def wload(col0, ncols):
            sl = wst['i'] % 3
            wst['i'] += 1
            p.dma('pool', [(WS[sl][:, 0:4, 0:ncols], win_v[:, 0:4, col0:col0 + ncols]),
                           (WS[sl][:, 4:8, 0:ncols], win_v[:, 4:8, col0:col0 + ncols])],
                  writes=['WS%d' % sl], sem='D_WS%d' % sl)
            return sl

        nscr = norm_scratch('mx_', 512, O_SC)
        for g in range(4):
            rms_norm_tokens(1, g * 512, 512, HT[:, :, g * 512:(g + 1) * 512], 'mx_', nscr,
                            lambda c, g=g: ('HT', c, g), add_eng='pool')
        htk = lambda k, g: ('HT', k, g)

        LQK = mem.view('LQK', O_RQ, F32, [256])
        p.dma('sp', [(LQK, lqk_d)], writes=['LQK'], sem='D_LQK')
        PRD = mem.view('PRD', O_RQ + 1024, F32, [128])
        p.op('dve', lambda e: e.tensor_tensor(out=PRD[:, 0:64], in0=LQK[:, 0:64], in1=LQK[:, 64:128], op=ALU.mult),
             reads=['LQK'], writes=[('PRD', 0)])
        p.op('dve', lambda e: e.tensor_tensor(out=PRD[:, 64:128], in0=LQK[:, 128:192], in1=LQK[:, 192:256], op=ALU.mult),
             reads=['LQK'], writes=[('PRD', 1)])
        p.op('dve', lambda e: e.tensor_reduce(out=LTMP[:, 0:2], in_=PRD.rearrange("p (a b) -> p a b", a=2),
                                              axis=AX.X, op=ALU.add),
             reads=[('PRD', 0), ('PRD', 1)], writes=[('LTMP', 0)])
        p.op('act', lambda e: e.activation(out=LTMP[:, 2:4], in_=LTMP[:, 0:2], func=AF.Exp),
             reads=[('LTMP', 0)], writes=[('LTMP', 1)])
        p.op('dve', lambda e: e.tensor_tensor(out=LTMP[:, 4:5], in0=LTMP[:, 3:4], in1=LTMP[:, 2:3], op=ALU.subtract),
             reads=[('LTMP', 1)], writes=[('LTMP', 2)])
        p.op('dve', lambda e: e.tensor_scalar(out=NLAM, in0=LTMP[:, 4:5], scalar1=-LAM_INIT, scalar2=None, op0=ALU.add),
             reads=[('LTMP', 2)], writes=['NLAM'])
        p.op('dve', lambda e: e.tensor_scalar(out=GSUB, in0=GSUB, scalar1=1.0 - LAM_INIT, scalar2=None, op0=ALU.mult),
             reads=['GSUB'], writes=['GSUB'])
        p.op('act', lambda e: e.activation(out=LGN, in_=DEC, func=AF.Exp, scale=-1.0), reads=['DEC'], writes=['LGN'])
        p.op('act', lambda e: e.activation(out=LGN, in_=LGN, func=AF.Ln, bias=1.0), reads=['LGN'], writes=['LGN'])
        p.op('dve', lambda e: e.tensor_scalar(out=LG, in0=LGN, scalar1=-1.0, scalar2=None, op0=ALU.mult),
             reads=['LGN'], writes=['LG'])
        for m in range(14):
            p.op('act', lambda e, m=m: e.activation(out=SCT[:, :, m], in_=LGN, func=AF.Exp, scale=-128.0 * m),
                 reads=['LGN'], writes=[('SCT', m)])
        sct_keys = [('SCT', m) for m in range(14)]

        def make_tables(br, U):
            POSI = mem.view(U + 'POSI', O_RQ, I32, [S])
            POSF = mem.view(U + 'POSF', O_RQ + 8192, F32, [S])
            KI = mem.view(U + 'KI', O_RQ + 16384, I32, [S])
            M2 = mem.view(U + 'M2', O_RQ + 16384, F32, [S])
            COS = mem.view(U + 'COS', O_RT, F32, [S])
            SIN = mem.view(U + 'SIN', O_RT + 8192, F32, [S])
            kP, kF, kK, kM, kC, kS = U + 'POSI', U + 'POSF', U + 'KI', U + 'M2', U + 'COS', U + 'SIN'
            p.dma('sp', [(POSI, pos_d)], writes=[kP], sem='D_POSI')
            p.op('dve', lambda e: e.tensor_copy(out=POSF, in_=POSI), reads=[kP], writes=[kF])
            p.op('dve', lambda e: e.tensor_scalar(out=COS, in0=POSF, scalar1=INV[:, br:br + 1], scalar2=None,
                                                  op0=ALU.mult), reads=[kF, 'INV'], writes=[kC])
            p.op('dve', lambda e: e.tensor_scalar(out=KI, in0=COS, scalar1=1.0 / TWO_PI, scalar2=None, op0=ALU.mult),
                 reads=[kC], writes=[kK])
            p.op('dve', lambda e: e.tensor_copy(out=POSF, in_=KI), reads=[kK], writes=[kF])
            p.op('dve', lambda e: e.scalar_tensor_tensor(out=COS, in0=POSF, scalar=-CW1, in1=COS, op0=ALU.mult,
                                                         op1=ALU.add), reads=[kF, kC], writes=[kC])
            p.op('dve', lambda e: e.scalar_tensor_tensor(out=COS, in0=POSF, scalar=-CW2, in1=COS, op0=ALU.mult,
                                                         op1=ALU.add), reads=[kF, kC], writes=[kC])

            def wrap(dst, kd, srcap, ksrc, shift):
                p.op('dve', lambda e: e.tensor_scalar(out=dst, in0=srcap, scalar1=shift, scalar2=None, op0=ALU.add),
                     reads=[ksrc], writes=[kd])
                p.op('dve', lambda e: e.tensor_scalar(out=POSF, in0=dst, scalar1=PI_LO, scalar2=None, op0=ALU.is_gt),
                     reads=[kd], writes=[kF])
                p.op('dve', lambda e: e.tensor_scalar(out=M2, in0=dst, scalar1=-PI_LO, scalar2=None, op0=ALU.is_lt),
                     reads=[kd], writes=[kM])
                p.op('dve', lambda e: e.scalar_tensor_tensor(out=dst, in0=POSF, scalar=-TWO_PI, in1=dst, op0=ALU.mult,
                                                             op1=ALU.add), reads=[kF, kd], writes=[kd])
                p.op('dve', lambda e: e.scalar_tensor_tensor(out=dst, in0=M2, scalar=TWO_PI, in1=dst, op0=ALU.mult,
                                                             op1=ALU.add), reads=[kM, kd], writes=[kd])
                p.op('dve', lambda e: e.tensor_scalar(out=dst, in0=dst, scalar1=PI_LO, scalar2=-PI_LO, op0=ALU.min,
                                                      op1=ALU.max), reads=[kd], writes=[kd])
            wrap(SIN, kS, COS, kC, 0.0)
            wrap(COS, kC, COS, kC, math.pi / 2)
            p.op('act', lambda e: e.activation(out=SIN, in_=SIN, func=AF.Sin, scale=SGN[:, 0:1]),
                 reads=[kS, 'SGN'], writes=[kS])
            p.op('act', lambda e: e.activation(out=COS, in_=COS, func=AF.Sin), reads=[kC], writes=[kC])
            return COS, SIN, kC, kS

        uq = {'u': 0}

        def qk_chunk(U, sl, wc0, dsts, dkeyf, do_norm, gain_col, tabs, scr):
            COS, SIN, kC, kS = tabs
            SQs, RSs, QN, T1s = scr
            for g in range(4):
                u = uq['u']; uq['u'] += 1
                b = u % 2
                SQ, RS, T1 = SQs[b], RSs[b], T1s[b]
                kq, kr, kt1 = U + 'SQ%d' % b, U + 'RS%d' % b, U + 'T1%d' % b
                zk, sk, rk = 'ps%d' % b, 'ps%d' % (2 + b), 'ps%d' % (4 + b)
                Z, SSp, ROT = ps[b], ps[2 + b], ps[4 + b]
                tk = slice(g * 512, (g + 1) * 512)
                for k in range(NCH):
                    p.op('pe', lambda e, Z=Z, k=k, tk=tk: e.matmul(Z[:, :], lhsT=WS[sl][:, k, wc0:wc0 + 128],
                                                                   rhs=HT[:, k, tk], start=(k == 0), stop=(k == NCH - 1)),
                         reads=['WS%d' % sl, htk(k, g)], writes=[zk], mark=(k == NCH - 1))
                qn = QN[:, b, :]
                qk_ = (U + 'QN', b)
                if do_norm:
                    p.op('act', lambda e, Z=Z, SQ=SQ: e.activation(out=SQ, in_=Z[:, :], func=AF.Square),
                         reads=[zk], writes=[kq])
                    p.op('pe', lambda e, SSp=SSp, SQ=SQ: e.matmul(SSp[:, :], lhsT=BLKF, rhs=SQ, start=True, stop=True),
                         reads=[kq, 'BLKF'], writes=[sk])
                    p.op('act', lambda e, SSp=SSp, RS=RS: e.activation(out=RS, in_=SSp[:, :], func=AF.Ln, scale=1.0 / 64,
                                                                        bias=EPSC), reads=[sk, 'EPSC'], writes=[kr])
                    p.op('act', lambda e, RS=RS: e.activation(out=RS, in_=RS, func=AF.Exp, scale=-0.5),
                         reads=[kr], writes=[kr])
                    p.op('dve', lambda e, Z=Z, qn=qn, RS=RS: e.scalar_tensor_tensor(
                        out=qn, in0=Z[:, :], scalar=GQK[:, gain_col:gain_col + 1], in1=RS, op0=ALU.mult, op1=ALU.mult),
                        reads=[zk, 'GQK', kr], writes=[qk_])
                else:
                    p.op('act', lambda e, Z=Z, qn=qn: e.copy(out=qn, in_=Z[:, :]), reads=[zk], writes=[qk_])
                p.op('pe', lambda e, ROT=ROT, qn=qn: e.matmul(ROT[:, :], lhsT=RMB, rhs=qn, start=True, stop=True),
                     reads=[qk_, 'RMB'], writes=[rk])
                p.op('dve', lambda e, qn=qn, tk=tk, T1=T1: e.tensor_tensor(out=T1, in0=qn, in1=COS[:, tk], op=ALU.mult),
                     reads=[qk_, kC], writes=[kt1])
                p.op('dve', lambda e, ROT=ROT, tk=tk, SQ=SQ: e.tensor_tensor(out=SQ, in0=ROT[:, :], in1=SIN[:, tk], op=ALU.mult),
                     reads=[rk, kS], writes=[kq])
                for (p0, p1, dst) in dsts:
                    p.op('dve', lambda e, tk=tk, T1=T1, SQ=SQ, p0=p0, p1=p1, dst=dst: e.tensor_tensor(
                        out=dst[:, tk], in0=T1[p0:p1, :], in1=SQ[p0:p1, :], op=ALU.add),
                        reads=[kt1, kq], writes=[dkeyf(g)])

        def qk_scratch(U):
            SQs = [mem.view(U + 'SQ%d' % i, O_SC + i * 8192, F32, [512]) for i in range(2)]
            RSs = [mem.view(U + 'RS%d' % i, O_SC + i * 8192 + 2048, F32, [512]) for i in range(2)]
            QN = mem.view(U + 'QN', O_SC + 4096, BF16, [2, 512])
            T1s = [mem.view(U + 'T1%d' % i, O_SC + 6144 + i * 6144, F32, [512]) for i in range(2)]
            return SQs, RSs, QN, T1s

        pend = []

        def flush():
            for f in pend:
                f()
            del pend[:]

        ps_bf = [ps[6].bitcast(BF16), ps[7].bitcast(BF16)]
        tcount = {'i': 0}

        def defer_transpose(src_ap, src_key, dst_ap, dst_key):
            def f():
                i = tcount['i']; tcount['i'] += 1
                b = i % 2
                pk = 'ps%d' % (6 + b)
                p.op('pe', lambda e: e.transpose(ps_bf[b][:, 0:128], src_ap, IDB), reads=[src_key, 'IDB'], writes=[pk])
                if i % 2 == 0:
                    p.op('act', lambda e: e.copy(out=dst_ap, in_=ps_bf[b][:, 0:128]), reads=[pk], writes=[dst_key])
                else:
                    p.op('dve', lambda e: e.tensor_copy(out=dst_ap, in_=ps_bf[b][:, 0:128]), reads=[pk], writes=[dst_key])
            pend.append(f)

        tabs = make_tables(0, 'da_')
        for pr in range(2):
            U = 'da%d_' % pr
            QZ = mem.view(U + 'QZ', O_Y, BF16, [2, 2, S])
            p.op('dve', lambda e, QZ=QZ: e.memset(QZ, 0.0), writes=[(U + 'QZ', hl_, g_) for hl_ in range(2) for g_ in range(4)])
            KT = mem.view(U + 'KT', O_RQ + 8192, BF16, [2, S])
            VA = mem.view(U + 'VA', O_RQ + 16384, BF16, [NT, 2, 130])
            scr = qk_scratch(U)
            slq = wload(0 + pr * 256, 256)
            for hl in range(2):
                qk_chunk(U, slq, hl * 128, [(0, 64, QZ[0:64, hl, 0, :]), (64, 128, QZ[64:128, hl, 1, :])],
                         lambda g, hl=hl: (U + 'QZ', hl, g), True, 0, tabs, scr)
            slk = wload(512 + pr * 256, 256)
            for hl in range(2):
                qk_chunk(U, slk, hl * 128, [(0, 128, KT[:, hl, :])], lambda g, hl=hl: (U + 'KT', hl, g), True, 1, tabs, scr)
            slv = wload(1024 + pr * 256, 256)
            p.op('dve', lambda e, VA=VA: e.memset(VA[:, :, :, 128:130], 1.0), writes=[(U + 'VA', 'ones')])
            for tt in range(NT):
                b = tt % 2
                pk = 'ps%d' % (6 + b)
                for k in range(NCH):
                    p.op('pe', lambda e, b=b, k=k, tt=tt, slv=slv: e.matmul(ps[6 + b][:, 0:256], lhsT=HT[:, k, tt * 128:(tt + 1) * 128],
                                                                 rhs=WS[slv][:, k, 0:256], start=(k == 0), stop=(k == NCH - 1)),
                         reads=['WS%d' % slv, htk(k, tt // 4)], writes=[pk], mark=(k == NCH - 1))
                srcv = ps[6 + b][:, 0:256].rearrange("p (h d) -> p h d", h=2)
                if tt % 2 == 0:
                    p.op('act', lambda e, srcv=srcv, tt=tt, VA=VA: e.copy(out=VA[:, tt, :, 0:128], in_=srcv),
                         reads=[pk], writes=[(U + 'VA', tt)])
                else:
                    p.op('dve', lambda e, srcv=srcv, tt=tt, VA=VA: e.tensor_copy(out=VA[:, tt, :, 0:128], in_=srcv),
                         reads=[pk], writes=[(U + 'VA', tt)])
            PT = mem.view(U + 'PT', O_SC + 13312, BF16, [2, 2, 256])
            EO = O_SC + 15360
            RC = mem.view(U + 'RC', EO, F32, [2, 4])
            T1e = mem.view(U + 'T1e', EO + 64, F32, [2, 128])
            Oe = mem.view(U + 'Oe', EO + 64 + 1024, F32, [2, 128])
            JK = mem.view(U + 'JK', EO + 64 + 2048, F32, [128])
            ONe = mem.view(U + 'ONe', EO + 64 + 2560, BF16, [2, 128])
            def da_qk(hl, qg, kt, QZ=QZ, KT=KT, PT=PT, U=U):
                b = kt % 2
                for m in range(2):
                    pk = 'ps%d' % (b * 2 + m)
                    p.op('pe', lambda e, b=b, m=m, kt=kt, qg=qg, hl=hl, QZ=QZ, KT=KT: e.matmul(
                        ps[b * 2 + m][:, 0:256], lhsT=KT[:, hl, kt * 128:(kt + 1) * 128],
                        rhs=QZ[:, hl, m, qg * 256:(qg + 1) * 256], start=True, stop=True),
                        reads=[(U + 'KT', hl, kt // 4), (U + 'QZ', hl, qg // 2)], writes=[pk])
                sview = pp[b][:, :].rearrange("p (m c) -> p m c", m=2)[:, :, 0:256]
                p.op('act', lambda e, b=b, PT=PT, sview=sview: e.activation(out=PT[:, b, :, :], in_=sview,
                                                                            func=AF.Exp, scale=0.125),
                     reads=['ps%d' % (b * 2), 'ps%d' % (b * 2 + 1)], writes=[(U + 'PT', b, 0), (U + 'PT', b, 1)])

            ACS = mem.view(U + 'ACS', O_SC, F32, [2, 260])

            def da_pv(hl, qg, kt, pr=pr, KT=KT, PT=PT, VA=VA, RC=RC, T1e=T1e, Oe=Oe, JK=JK, ONe=ONe, U=U, ACS=ACS):
                b = kt % 2
                hd = pr * 2 + hl
                if kt == 4:
                    flush()
                for qt in range(2):
                    for m in range(2):
                        first = (kt == 0 and m == 0)
                        p.op('pe', lambda e, qt=qt, m=m, b=b, kt=kt, hl=hl, first=first, PT=PT, VA=VA: e.matmul(
                            ps[4 + qt][:, m * 130:m * 130 + 129], lhsT=PT[:, b, m, qt * 128:(qt + 1) * 128],
                            rhs=VA[:, kt, hl, 0:129], start=first, stop=(kt == NT - 1), skip_group_check=True),
                            reads=[(U + 'PT', b, m), (U + 'VA', kt), (U + 'VA', 'ones')], writes=['ps%d' % (4 + qt)],
                            mark=(m == 1))
                if kt != NT - 1:
                    return
                for qt in range(2):
                    if qt == 0:
                        p.op('act', lambda e, qt=qt, ACS=ACS: e.copy(out=ACS[:, qt, :], in_=ps[4 + qt][:, 0:260]),
                             reads=['ps%d' % (4 + qt)], writes=[(U + 'ACS', qt)])
                    else:
                        p.op('dve', lambda e, qt=qt, ACS=ACS: e.tensor_copy(out=ACS[:, qt, :], in_=ps[4 + qt][:, 0:260]),
                             reads=['ps%d' % (4 + qt)], writes=[(U + 'ACS', qt)])
                for qt in range(2):
                    ak = (U + 'ACS', qt)
                    acc = ACS[:, qt, :]
                    sums = acc.rearrange("p (m c) -> p m c", m=2)[:, :, 128:129]
                    p.op('dve', lambda e, qt=qt, sums=sums, RC=RC: e.reciprocal(out=RC[:, qt, 0:2].rearrange("p (m o) -> p m o", o=1), in_=sums),
                         reads=[ak], writes=[(U + 'RC', qt)])
                    p.op('dve', lambda e, qt=qt, RC=RC: e.tensor_scalar(out=RC[:, qt, 2:3], in0=RC[:, qt, 1:2], scalar1=NLAM[:, 0:1],
                                                                     scalar2=None, op0=ALU.mult),
                         reads=[(U + 'RC', qt), 'NLAM'], writes=[(U + 'RC', qt)])
                    p.op('dve', lambda e, qt=qt, acc=acc, RC=RC, T1e=T1e: e.tensor_scalar(
                        out=T1e[:, qt, :], in0=acc[:, 130:258], scalar1=RC[:, qt, 2:3], scalar2=None, op0=ALU.mult),
                        reads=[ak, (U + 'RC', qt)], writes=[(U + 'T1e', qt)])
                    p.op('dve', lambda e, qt=qt, acc=acc, RC=RC, T1e=T1e, Oe=Oe: e.scalar_tensor_tensor(
                        out=Oe[:, qt, :], in0=acc[:, 0:128], scalar=RC[:, qt, 0:1], in1=T1e[:, qt, :],
                        op0=ALU.mult, op1=ALU.add),
                        reads=[ak, (U + 'RC', qt), (U + 'T1e', qt)], writes=[(U + 'Oe', qt)])
                for qt in range(2):
                    p.op('act', lambda e, qt=qt, Oe=Oe, JK=JK, RC=RC: e.activation(out=JK, in_=Oe[:, qt, :], func=AF.Square,
                                                                                accum_out=RC[:, qt, 3:4]),
                         reads=[(U + 'Oe', qt)], writes=[U + 'JK', (U + 'RC', qt, 's')])
                    p.op('act', lambda e, qt=qt, RC=RC: e.activation(out=RC[:, qt, 3:4], in_=RC[:, qt, 3:4], func=AF.Ln,
                                                                   scale=1.0 / 128, bias=EPSC),
                         reads=[(U + 'RC', qt, 's'), 'EPSC'], writes=[(U + 'RC', qt, 's')])
                    p.op('act', lambda e, qt=qt, RC=RC: e.activation(out=RC[:, qt, 3:4], in_=RC[:, qt, 3:4], func=AF.Exp,
                                                                   scale=-0.5),
                         reads=[(U + 'RC', qt, 's')], writes=[(U + 'RC', qt, 's')])
                for qt in range(2):
                    tq = qg * 2 + qt
                    p.op('dve', lambda e, qt=qt, Oe=Oe, RC=RC, ONe=ONe: e.scalar_tensor_tensor(
                        out=ONe[:, qt, :], in0=Oe[:, qt, :], scalar=RC[:, qt, 3:4], in1=GSUB, op0=ALU.mult, op1=ALU.mult),
                        reads=[(U + 'Oe', qt), (U + 'RC', qt, 's'), 'GSUB'], writes=[(U + 'ONe', qt)])
                    defer_transpose(ONe[:, qt, :], (U + 'ONe', qt), OAT[:, hd, tq * 128:(tq + 1) * 128], ('OAT', hd, tq // 4))

            its = [(hl, qg, kt) for hl in range(2) for qg in range(8) for kt in range(NT)]
            da_qk(*its[0])
            for i, it in enumerate(its):
                if i + 1 < len(its):
                    da_qk(*its[i + 1])
                da_pv(*it)
            flush()

        if STAGE == 2:
            return

        YT = mem.view('YT', O_Y, BF16, [4, S])
        tabs = make_tables(1, 'rt_')
        for pr in range(2):
            U = 'rt%d_' % pr
            QT = mem.view(U + 'QT', O_RQ, BF16, [S])
            KT = mem.view(U + 'KT', O_RQ + 4096, BF16, [S])
            VR = mem.view(U + 'VR', O_RQ + 8192, BF16, [NT, 256])
            SG = mem.view(U + 'SG', O_RQ + 16384, BF16, [NT, 256])
            scr = qk_scratch(U)
            slq = wload(1536 + pr * 128, 128)
            qk_chunk(U, slq, 0, [(0, 128, QT)], lambda g: (U + 'QT', g), False, 0, tabs, scr)
            slk = wload(1792 + pr * 128, 128)
            qk_chunk(U, slk, 0, [(0, 128, KT)], lambda g: (U + 'KT', g), False, 0, tabs, scr)
            slv = wload(2048 + pr * 256, 256)
            for tt in range(NT):
                b = tt % 2
                pk = 'ps%d' % (6 + b)
                for k in range(NCH):
                    p.op('pe', lambda e, b=b, k=k, tt=tt, slv=slv: e.matmul(ps[6 + b][:, 0:256], lhsT=HT[:, k, tt * 128:(tt + 1) * 128],
                                                                 rhs=WS[slv][:, k, 0:256], start=(k == 0), stop=(k == NCH - 1)),
                         reads=['WS%d' % slv, htk(k, tt // 4)], writes=[pk], mark=(k == NCH - 1))
                if tt % 2 == 0:
                    p.op('act', lambda e, b=b, tt=tt, VR=VR: e.copy(out=VR[:, tt, :], in_=ps[6 + b][:, 0:256]),
                         reads=[pk], writes=[(U + 'VR', tt)])
                else:
                    p.op('dve', lambda e, b=b, tt=tt, VR=VR: e.tensor_copy(out=VR[:, tt, :], in_=ps[6 + b][:, 0:256]),
                         reads=[pk], writes=[(U + 'VR', tt)])
            slg = wload(2560 + pr * 256, 256)
            for tt in range(NT):
                b = tt % 2
                pk = 'ps%d' % (6 + b)
                for k in range(NCH):
                    p.op('pe', lambda e, b=b, k=k, tt=tt, slg=slg: e.matmul(ps[6 + b][:, 0:256], lhsT=HT[:, k, tt * 128:(tt + 1) * 128],
                                                                 rhs=WS[slg][:, k, 0:256], start=(k == 0), stop=(k == NCH - 1)),
                         reads=['WS%d' % slg, htk(k, tt // 4)], writes=[pk], mark=(k == NCH - 1))
                p.op('act', lambda e, b=b, tt=tt, SG=SG: e.activation(out=SG[:, tt, :], in_=ps[6 + b][:, 0:256], func=AF.Silu),
                     reads=[pk], writes=[(U + 'SG', tt)])
            ET = mem.view(U + 'ET', O_SC, F32, [640])
            DG = mem.view(U + 'DG', O_SC + 2560, F32, [640])
            TX = mem.view(U + 'TX', O_SC + 5120, F32, [640])
            TMk = mem.view(U + 'TMk', O_SC + 8192, F32, [2, 640])
            p.dma('sp', [(ET, etab_d), (DG, diag2_d)], writes=[U + 'ET', U + 'DG'], sem='D_ETDG')
            for hl in range(2):
                h = pr * 2 + hl
                p.op('dve', lambda e, hl=hl, h=h, TMk=TMk, ET=ET: e.tensor_scalar(
                    out=TMk[:, hl, :], in0=ET, scalar1=0.0, scalar2=LG[:, h:h + 1], op0=ALU.max, op1=ALU.mult),
                    reads=[U + 'ET', 'LG'], writes=[(U + 'TMk', hl)])
                p.op('dve', lambda e, h=h, TX=TX, ET=ET: e.tensor_scalar(
                    out=TX, in0=ET, scalar1=0.0, scalar2=LGN[:, 4 + h:5 + h], op0=ALU.min, op1=ALU.mult),
                    reads=[U + 'ET', 'LGN'], writes=[U + 'TX'])
                p.op('dve', lambda e, hl=hl, TMk=TMk, TX=TX: e.tensor_tensor(out=TMk[:, hl, :], in0=TMk[:, hl, :], in1=TX, op=ALU.add),
                     reads=[U + 'TX', (U + 'TMk', hl)], writes=[(U + 'TMk', hl)])
                p.op('act', lambda e, hl=hl, TMk=TMk: e.activation(out=TMk[:, hl, :], in_=TMk[:, hl, :], func=AF.Exp),
                     reads=[(U + 'TMk', hl)], writes=[(U + 'TMk', hl)])
                p.op('dve', lambda e, hl=hl, TMk=TMk, DG=DG: e.tensor_tensor(out=TMk[:, hl, :], in0=TMk[:, hl, :], in1=DG, op=ALU.mult),
                     reads=[U + 'DG', (U + 'TMk', hl)], writes=[(U + 'TMk', hl)])
            PT = mem.view(U + 'PT', O_SC + 13312, BF16, [2, 2, 256])
            EO = O_SC + 15360
            RC = mem.view(U + 'RC', EO, F32, [2, 2, 2])
            YN = mem.view(U + 'YN', EO + 64, F32, [4, 128])
            JK = mem.view(U + 'JK', EO + 64 + 2048, F32, [128])
            YG = mem.view(U + 'YG', EO + 64 + 2560, BF16, [4, 128])
            def rt_qk(qg, kt, pr=pr, QT=QT, KT=KT, PT=PT, TMk=TMk, U=U):
                b = kt % 2
                dl = 2 * qg - kt
                for hl in range(2):
                    h = pr * 2 + hl
                    pk = 'ps%d' % (b * 2 + hl)
                    p.op('pe', lambda e, b=b, hl=hl, kt=kt, qg=qg, QT=QT, KT=KT: e.matmul(
                        ps[b * 2 + hl][:, 0:256], lhsT=KT[hl * 64:(hl + 1) * 64, kt * 128:(kt + 1) * 128],
                        rhs=QT[hl * 64:(hl + 1) * 64, qg * 256:(qg + 1) * 256], start=True, stop=True),
                        reads=[(U + 'KT', kt // 4), (U + 'QT', qg // 2)], writes=[pk])
                    if dl >= 1:
                        c0, sc = 384, SCT[:, h, dl - 1:dl]
                    elif dl <= -2:
                        c0, sc = 0, SCT[:, 4 + h, -dl - 2:-dl - 1]
                    else:
                        c0, sc = (256 if dl == 0 else 128), None
                    if sc is not None:
                        p.op('dve', lambda e, b=b, hl=hl, c0=c0, sc=sc, PT=PT, TMk=TMk: e.scalar_tensor_tensor(
                            out=PT[:, b, hl, :], in0=ps[b * 2 + hl][:, 0:256], scalar=sc, in1=TMk[:, hl, c0:c0 + 256],
                            op0=ALU.mult, op1=ALU.mult),
                            reads=[pk, (U + 'TMk', hl)] + sct_keys, writes=[(U + 'PT', b, hl)])
                    else:
                        p.op('dve', lambda e, b=b, hl=hl, c0=c0, PT=PT, TMk=TMk: e.tensor_tensor(
                            out=PT[:, b, hl, :], in0=ps[b * 2 + hl][:, 0:256], in1=TMk[:, hl, c0:c0 + 256], op=ALU.mult),
                            reads=[pk, (U + 'TMk', hl)], writes=[(U + 'PT', b, hl)])

            ACR = mem.view(U + 'ACR', O_SC, F32, [2, 256])

            def rt_pv(qg, kt, pr=pr, PT=PT, VR=VR, RC=RC, YN=YN, JK=JK, YG=YG, SG=SG, U=U, ACR=ACR):
                b = kt % 2
                if kt == 4:
                    flush()
                for qt in range(2):
                    for hl in range(2):
                        first = (kt == 0 and hl == 0)
                        p.op('pe', lambda e, qt=qt, hl=hl, b=b, kt=kt, first=first, PT=PT, VR=VR: e.matmul(
                            ps[4 + qt][:, hl * 128:(hl + 1) * 128], lhsT=PT[:, b, hl, qt * 128:(qt + 1) * 128],
                            rhs=VR[:, kt, hl * 128:(hl + 1) * 128], start=first, stop=(kt == NT - 1), skip_group_check=True),
                            reads=[(U + 'PT', b, hl), (U + 'VR', kt)], writes=['ps%d' % (4 + qt)], mark=(hl == 1))
                if kt != NT - 1:
                    return
                for qt in range(2):
                    if qt == 0:
                        p.op('act', lambda e, qt=qt, ACR=ACR: e.copy(out=ACR[:, qt, :], in_=ps[4 + qt][:, 0:256]),
                             reads=['ps%d' % (4 + qt)], writes=[(U + 'ACR', qt)])
                    else:
                        p.op('dve', lambda e, qt=qt, ACR=ACR: e.tensor_copy(out=ACR[:, qt, :], in_=ps[4 + qt][:, 0:256]),
                             reads=['ps%d' % (4 + qt)], writes=[(U + 'ACR', qt)])
                for qt in range(2):
                    ak = (U + 'ACR', qt)
                    for hl in range(2):
                        Y = ACR[:, qt, hl * 128:(hl + 1) * 128]
                        sk_ = (U + 'RC', qt, hl)
                        p.op('act', lambda e, Y=Y, qt=qt, hl=hl, JK=JK, RC=RC: e.activation(out=JK, in_=Y, func=AF.Square,
                                                                                         accum_out=RC[:, qt, hl, 0:1]),
                             reads=[ak], writes=[U + 'JK', sk_])
                        p.op('act', lambda e, qt=qt, hl=hl, RC=RC: e.activation(out=RC[:, qt, hl, 0:1], in_=RC[:, qt, hl, 0:1],
                                                                              func=AF.Ln, scale=1.0 / 128, bias=EPSC),
                             reads=[sk_, 'EPSC'], writes=[sk_])
                        p.op('act', lambda e, qt=qt, hl=hl, RC=RC: e.activation(out=RC[:, qt, hl, 0:1], in_=RC[:, qt, hl, 0:1],
                                                                              func=AF.Exp, scale=-0.5),
                             reads=[sk_], writes=[sk_])
                for qt in range(2):
                    ak = (U + 'ACR', qt)
                    tq = qg * 2 + qt
                    for hl in range(2):
                        h = pr * 2 + hl
                        Y = ACR[:, qt, hl * 128:(hl + 1) * 128]
                        sk_ = (U + 'RC', qt, hl)
                        yi = qt * 2 + hl
                        p.op('dve', lambda e, Y=Y, qt=qt, hl=hl, yi=yi, RC=RC, YN=YN: e.scalar_tensor_tensor(
                            out=YN[:, yi, :], in0=Y, scalar=RC[:, qt, hl, 0:1], in1=GRET, op0=ALU.mult, op1=ALU.mult),
                            reads=[ak, sk_, 'GRET'], writes=[(U + 'YN', yi)])
                        p.op('dve', lambda e, hl=hl, tq=tq, yi=yi, YN=YN, YG=YG, SG=SG: e.tensor_tensor(
                            out=YG[:, yi, :], in0=YN[:, yi, :], in1=SG[:, tq, hl * 128:(hl + 1) * 128], op=ALU.mult),
                            reads=[(U + 'YN', yi), (U + 'SG', tq)], writes=[(U + 'YG', yi)])
                        defer_transpose(YG[:, yi, :], (U + 'YG', yi), YT[:, h, tq * 128:(tq + 1) * 128], ('YT', h, tq // 4))

            its = [(qg, kt) for qg in range(8) for kt in range(NT)]
            rt_qk(*its[0])
            for i, it in enumerate(its):
                if i + 1 < len(its):
                    rt_qk(*its[i + 1])
                rt_pv(*it)
            flush()

        if STAGE == 3:
            return

        MG = mem.view('MG', O_RT, BF16, [NCH, S])
        MWA = [mem.view('MWA%d' % i, O_W + i * 6144, BF16, [NCH, 128]) for i in range(2)]
        MWR = [mem.view('MWR%d' % i, O_W + i * 6144 + 2048, BF16, [NCH, 128]) for i in range(2)]
        MBA = [mem.view('MBA%d' % i, O_W + i * 6144 + 4096, BF16, [4, 128]) for i in range(2)]
        MBR = [mem.view('MBR%d' % i, O_W + i * 6144 + 5120, BF16, [4, 128]) for i in range(2)]
        SA = [mem.view('SA%d' % i, O_SC + i * 2048, F32, [512]) for i in range(2)]
        SR = [mem.view('SR%d' % i, O_SC + 4096 + i * 2048, F32, [512]) for i in range(2)]
        M1 = [mem.view('M1_%d' % i, O_SC + 8192 + i * 2048, F32, [512]) for i in range(2)]
        M2 = [mem.view('M2_%d' % i, O_SC + 12288 + i * 2048, F32, [512]) for i in range(2)]
        wba_v = wba_d.rearrange("(c p) n -> p c n", p=128)
        wbr_v = wbr_d.rearrange("(c p) n -> p c n", p=128)
        wout_v = wout_d.rearrange("(c p) n -> p c n", p=128)
        it = 0
        for d in range(NCH):
            sl = d % 2
            wk = ['MWA%d' % sl, 'MWR%d' % sl, 'MBA%d' % sl, 'MBR%d' % sl]
            p.dma('pool', [(MWA[sl][:, 0:4, :], win_v[:, 0:4, 3072 + d * 128:3072 + (d + 1) * 128]),
                           (MWA[sl][:, 4:8, :], win_v[:, 4:8, 3072 + d * 128:3072 + (d + 1) * 128]),
                           (MWR[sl][:, 0:4, :], win_v[:, 0:4, 4096 + d * 128:4096 + (d + 1) * 128]),
                           (MWR[sl][:, 4:8, :], win_v[:, 4:8, 4096 + d * 128:4096 + (d + 1) * 128]),
                           (MBA[sl], wba_v[:, :, d * 128:(d + 1) * 128]),
                           (MBR[sl], wbr_v[:, :, d * 128:(d + 1) * 128])], writes=wk, sem='D_MW%d' % sl)
            for tq in range(4):
                b = it % 2; it += 1
                tk = slice(tq * 512, (tq + 1) * 512)
                A, Bp, C, Dp = ps[b * 4], ps[b * 4 + 1], ps[b * 4 + 2], ps[b * 4 + 3]
                ka, kb, kc, kd = ['ps%d' % (b * 4 + i) for i in range(4)]
                for c in range(4):
                    p.op('pe', lambda e, A=A, c=c, tk=tk, sl=sl: e.matmul(A[:, :], lhsT=MBA[sl][:, c, :], rhs=OAT[:, c, tk],
                                                                       start=(c == 0), stop=(c == 3)),
                         reads=[wk[2], ('OAT', c, tq)], writes=[ka], mark=(c == 3))
                for c in range(4):
                    p.op('pe', lambda e, Bp=Bp, c=c, tk=tk, sl=sl: e.matmul(Bp[:, :], lhsT=MBR[sl][:, c, :], rhs=YT[:, c, tk],
                                                                         start=(c == 0), stop=(c == 3)),
                         reads=[wk[3], ('YT', c, tq)], writes=[kb], mark=(c == 3))
                for k in range(NCH):
                    p.op('pe', lambda e, C=C, k=k, tk=tk, sl=sl: e.matmul(C[:, :], lhsT=MWA[sl][:, k, :], rhs=HT[:, k, tk],
                                                                       start=(k == 0), stop=(k == NCH - 1)),
                         reads=[wk[0], htk(k, tq)], writes=[kc], mark=(k == NCH - 1))
                for k in range(NCH):
                    p.op('pe', lambda e, Dp=Dp, k=k, tk=tk, sl=sl: e.matmul(Dp[:, :], lhsT=MWR[sl][:, k, :], rhs=HT[:, k, tk],
                                                                         start=(k == 0), stop=(k == NCH - 1)),
                         reads=[wk[1], htk(k, tq)], writes=[kd], mark=(k == NCH - 1))
                p.op('act', lambda e, C=C, b=b, d=d: e.activation(out=SA[b], in_=C[:, :], func=AF.Sigmoid, bias=BM[:, d:d + 1]),
                     reads=[kc, 'BM'], writes=['SA%d' % b])
                p.op('act', lambda e, Dp=Dp, b=b, d=d: e.activation(out=SR[b], in_=Dp[:, :], func=AF.Sigmoid, bias=BM[:, 8 + d:9 + d]),
                     reads=[kd, 'BM'], writes=['SR%d' % b])
                p.op('dve', lambda e, A=A, b=b: e.tensor_tensor(out=M1[b], in0=A[:, :], in1=SA[b], op=ALU.mult),
                     reads=[ka, 'SA%d' % b], writes=['M1_%d' % b])
                p.op('dve', lambda e, Bp=Bp, b=b: e.tensor_tensor(out=M2[b], in0=Bp[:, :], in1=SR[b], op=ALU.mult),
                     reads=[kb, 'SR%d' % b], writes=['M2_%d' % b])
                p.op('dve', lambda e, b=b, d=d, tk=tk: e.tensor_tensor(out=MG[:, d, tk], in0=M1[b], in1=M2[b], op=ALU.add),
                     reads=['M1_%d' % b, 'M2_%d' % b], writes=[('MG', d, tq)])
        WO = [mem.view('WO%d' % i, O_W + i * 4096, BF16, [NCH, 256]) for i in range(2)]
        it = 0
        for j in range(4):
            sl = j % 2
            p.dma('pool', [(WO[sl][:, 0:4, :], wout_v[:, 0:4, j * 256:(j + 1) * 256]),
                           (WO[sl][:, 4:8, :], wout_v[:, 4:8, j * 256:(j + 1) * 256])], writes=['WO%d' % sl], sem='D_WO%d' % sl)
            for dl in range(2):
                d = j * 2 + dl
                for tq in range(4):
                    pb = it % 2; it += 1
                    pk = 'ps%d' % pb
                    tk = slice(tq * 512, (tq + 1) * 512)
                    for k in range(NCH):
                        p.op('pe', lambda e, pb=pb, k=k, tk=tk, sl=sl, dl=dl: e.matmul(
                            ps[pb][:, :], lhsT=WO[sl][:, k, dl * 128:(dl + 1) * 128], rhs=MG[:, k, tk],
                            start=(k == 0), stop=(k == NCH - 1)),
                            reads=['WO%d' % sl, ('MG', k, tq)], writes=[pk], mark=(k == NCH - 1))
                    xs_ = XT[:, d, tk]
                    p.op('dve', lambda e, pb=pb, d=d, xs_=xs_: e.scalar_tensor_tensor(
                        out=xs_, in0=ps[pb][:, :], scalar=HG[:, 8 + d:9 + d], in1=xs_, op0=ALU.mult, op1=ALU.add),
                        reads=[pk, ('HG', 1), ('XT', d, tq)], writes=[('XT', d, tq)])

    if STAGE >= 2:
        mixer()
    if STAGE >= 5:
        ffn(2, f2w1_d, f2w3_d, f2w2_d)

    OS_OFF = OFF_FREE
    osb = [mem.view('OS%d' % i, OS_OFF + i * 16384, F32, [4, D]) for i in range(2)]
    o_v = out_d.rearrange("(t p) d -> p t d", p=128)
    for g in range(4):
        sl = g % 2
        key = 'OS%d' % sl
        for c in range(NCH):
            pb = (g * NCH + c) % 2
            pk = 'ps%d' % pb
            for t in range(4):
                p.op('pe', lambda e, g=g, t=t, c=c, pb=pb: e.transpose(
                    ps[pb][:, t * 128:(t + 1) * 128], XT[:, c, (g * 4 + t) * 128:(g * 4 + t + 1) * 128], IDF),
                    reads=[('XT', c, g), 'IDF'], writes=[pk], mark=(t == 3))
            src = ps[pb][:, :].rearrange("p (t f) -> p t f", t=4)
            dst = osb[sl][:, :, c * 128:(c + 1) * 128]
            if c % 2 == 0:
                p.op('dve', lambda e, src=src, dst=dst: e.tensor_copy(out=dst, in_=src),
                     reads=[pk], writes=[(key, c)])
            else:
                p.op('act', lambda e, src=src, dst=dst: e.copy(out=dst, in_=src),
                     reads=[pk], writes=[(key, c)])
        p.dma('sp', [(o_v[:, g * 4 + t:g * 4 + t + 1, :], osb[sl][:, t:t + 1, :]) for t in range(4)],
              reads=[(key, c) for c in range(NCH)], sem='D_st%d' % sl)

    p.finish()
    p.emit()
    return nc


_CACHE = {}


def _consts():
    return {
        "ident": np.eye(128, dtype=np.float32),
        "ones": np.ones((128, 128), dtype=np.float32),
        "blk": np.kron(np.eye(2, dtype=np.float32), np.ones((64, 64), dtype=np.float32)),
        "rmat": _rmat(),
        "inv": _inv(),
        "sgn": np.where((np.arange(128) % 64) < 32, -1.0, 1.0).astype(np.float32).reshape(128, 1),
        "etab": _etab(),
        "diag2": np.where(_etab() == 0, 0.25, 0.125).astype(np.float32),
    }


def _rmat():
    r = np.zeros((128, 128), dtype=np.float32)
    for m in range(128):
        src = (m // 64) * 64 + ((m % 64) + 32) % 64
        r[src, m] = 1.0
    return r


def _inv():
    j = (np.arange(128) % 32).astype(np.float32)
    inv0 = (1.0 / (np.float32(10000.0) ** (np.arange(0, 64, 2, dtype=np.float32) / np.float32(64)))).astype(np.float32)
    inv1 = (1.0 / (np.float32(10000.0) ** np.linspace(0.0, 1.0, 32, dtype=np.float32))).astype(np.float32)
    idx = np.arange(128) % 32
    return np.stack([inv0[idx], inv1[idx]], axis=1).astype(np.float32)


def _etab():
    c = np.arange(640, dtype=np.float32)[None, :]
    kk = np.arange(128, dtype=np.float32)[:, None]
    return (c - kk - 256.0).astype(np.float32)


def kernel(**inp):
    if 'nc' not in _CACHE:
        _CACHE['nc'] = build_program()
    nc = _CACHE['nc']
    f32 = np.float32
    A = lambda a: np.ascontiguousarray(a)
    cst = _consts()
    gains = np.concatenate([inp['norm_ffn1'][0].reshape(NCH, 128).T, inp['norm_mix'][0].reshape(NCH, 128).T,
                            inp['norm_ffn2'][0].reshape(NCH, 128).T], axis=1).astype(f32)
    shared = {
        "w_ada": A(inp['w_ada'][0]), "b_adaT": A(inp['b_ada'][0].reshape(72, 128).T),
        "gainsT": A(gains),
        "f1w1": A(inp['ffn1_w1'][0]), "f1w3": A(inp['ffn1_w3'][0]), "f1w2": A(inp['ffn1_w2'][0]),
        "f2w1": A(inp['ffn2_w1'][0]), "f2w3": A(inp['ffn2_w3'][0]), "f2w2": A(inp['ffn2_w2'][0]),
        "w_in": A(inp['w_in'][0]),
    }
    w_in = np.array(inp['w_in'][0], dtype=f32, copy=True)
    perm = np.concatenate([np.arange(0, 64, 2), np.arange(1, 64, 2)])
    for base in (1536, 1792):
        for h in range(4):
            blkc = w_in[:, base + h * 64: base + (h + 1) * 64]
            w_in[:, base + h * 64: base + (h + 1) * 64] = blkc[:, perm]
    shared["w_in"] = A(w_in)
    shared["gqk"] = A(np.stack([np.tile(inp['da_q_gain'][0], 2), np.tile(inp['da_k_gain'][0], 2)], axis=1).astype(f32))
    lqk = np.concatenate([inp['da_lambda_q1'][0], inp['da_lambda_k1'][0], inp['da_lambda_q2'][0], inp['da_lambda_k2'][0]])
    shared["lqk"] = A(np.broadcast_to(lqk[None, :], (128, 256)).astype(f32))
    shared["gsub"] = A(np.broadcast_to(inp['da_subln'][0][None, :], (128, 128)).astype(f32))
    shared["gret"] = A(np.broadcast_to(inp['ret_norm'][0][None, :], (128, 128)).astype(f32))
    dec = np.concatenate([inp['ret_decay_f'][0], inp['ret_decay_b'][0]])
    shared["dec"] = A(np.broadcast_to(dec[None, :], (128, 8)).astype(f32))
    shared["bmT"] = A(inp['b_merge'][0].reshape(16, 128).T.astype(f32))
    shared["w_ba"] = A(inp['w_branch_a'][0]); shared["w_br"] = A(inp['w_branch_r'][0]); shared["w_out"] = A(inp['w_out'][0])
    shared.update(cst)
    in_maps = []
    for b in range(NB):
        m = dict(shared)
        m["x"] = A(inp['x'][b])
        m["cT"] = A(inp['c'][b].reshape(NCH, 128).T)
        m["pos"] = A(np.broadcast_to(inp['positions'][b][None, :], (128, S))).astype(np.int32)
        in_maps.append(m)
    res = run_bass_kernel_spmd(nc, in_maps, core_ids=list(range(NB)))
    _CACHE['last'] = res
    return np.stack([r["out"] for r in res.results], axis=0).astype(np.float32)
```

```python
import math
import os
import numpy as np
import concourse.bass as bass
import concourse.mybir as mybir
from concourse.bass_utils import run_bass_kernel_spmd

F32 = mybir.dt.float32
BF16 = mybir.dt.bfloat16
I32 = mybir.dt.int32
AF = mybir.ActivationFunctionType
ALU = mybir.AluOpType
AX = mybir.AxisListType

D = 1024
S = 2048
NB = 8
DFF = 2816
NCH = D // 128
NFC = DFF // 128
NT = S // 128
EPS = 1e-6
INW = 5120
LAM_INIT = 0.8 - 0.6 * math.exp(-0.3 * 0)

STAGE = int(os.environ.get("MK_STAGE", "9"))


def _dsize(dt):
    return {F32: 4, BF16: 2, I32: 4}[dt]


class Prog:
    ENG = ('pe', 'act', 'dve', 'pool', 'sp')

    def __init__(self, nc):
        self.nc = nc
        self.q = {e: [] for e in self.ENG}
        self.sems = {}
        self.cnt = {}
        self.seen = {e: {} for e in self.ENG}
        self.bufs = {}
        self.regions = {}
        self.region_init = {}
        for e in ('pe', 'act', 'dve', 'pool'):
            self._sem('E_' + e)

    def _sem(self, name):
        if name not in self.sems:
            self.sems[name] = self.nc.alloc_semaphore(name)
            self.cnt[name] = 0
        return name

    def region(self, name, space, off, nbytes):
        assert name not in self.regions, name
        merged = {}
        for n2, (sp2, o2, b2) in self.regions.items():
            if sp2 == space and o2 < off + nbytes and off < o2 + b2:
                for kk, b in self.bufs.items():
                    nm = kk[0] if isinstance(kk, tuple) else kk
                    if nm == n2:
                        if b[0] is not None:
                            merged[b[0][0]] = max(merged.get(b[0][0], 0), b[0][1])
                        for s, v in b[1].items():
                            merged[s] = max(merged.get(s, 0), v)
                for s, v in self.region_init.get(n2, {}).items():
                    merged[s] = max(merged.get(s, 0), v)
        self.regions[name] = (space, off, nbytes)
        self.region_init[name] = merged

    def _get(self, k):
        b = self.bufs.get(k)
        if b is None:
            nm = k[0] if isinstance(k, tuple) else k
            init = self.region_init.get(nm)
            if init:
                return [None, dict(init)]
            return None
        return b

    def _need(self, e, tok, waits):
        if tok is None:
            return
        s, v = tok
        if e == 'pe' and s == 'E_pe':
            return
        if self.seen[e].get(s, 0) >= v:
            return
        waits[s] = max(waits.get(s, 0), v)

    def _deps(self, e, reads, writes):
        waits = {}
        for k in reads:
            b = self._get(k)
            if b:
                self._need(e, b[0], waits)
        for k in writes:
            b = self._get(k)
            if b:
                self._need(e, b[0], waits)
                for s, v in b[1].items():
                    self._need(e, (s, v), waits)
        for s, v in waits.items():
            self.seen[e][s] = v
            h = self.sems[s]
            self.q[e].append(lambda eng, h=h, v=v: eng.wait_ge(h, v))

    def _commit(self, tok, reads, writes):
        s, v = tok
        for k in reads:
            b = self.bufs.get(k)
            if b is None:
                g = self._get(k)
                b = self.bufs[k] = [None, dict(g[1])] if g else [None, {}]
            b[1][s] = max(b[1].get(s, 0), v)
        for k in writes:
            self.bufs[k] = [tok, {}]

    def op(self, e, fn, reads=(), writes=(), mark=True):
        self._deps(e, reads, writes)
        s = 'E_' + e
        if mark:
            self.cnt[s] += 1
            h = self.sems[s]
            self.q[e].append(lambda eng, fn=fn, h=h: fn(eng).then_inc(h, 1))
            tok = (s, self.cnt[s])
        else:
            self.q[e].append(lambda eng, fn=fn: fn(eng))
            tok = (s, self.cnt[s] + 1)
        self._commit(tok, reads, writes)

    def dma(self, qe, pieces, reads=(), writes=(), sem=None):
        self._deps(qe, reads, writes)
        if sem is None:
            k = (writes[0] if writes else reads[0])
            sem = 'D_' + str(k)
        s = self._sem(sem)
        h = self.sems[s]
        for (o, i) in pieces:
            self.cnt[s] += 16
            self.q[qe].append(lambda eng, o=o, i=i, h=h: eng.dma_start(out=o, in_=i).then_inc(h, 16))
        tok = (s, self.cnt[s])
        self._commit(tok, reads, writes)

    def finish(self):
        for e in self.ENG:
            for s, c in self.cnt.items():
                if c > 0 and self.seen[e].get(s, 0) < c and s != 'E_' + e:
                    h = self.sems[s]
                    self.q[e].append(lambda eng, h=h, c=c: eng.wait_ge(h, c))
                    self.seen[e][s] = c

    def emit(self):
        nc = self.nc
        with nc.Block() as block:
            @block.tensor
            def _(eng):
                for f in self.q['pe']:
                    f(eng)

            @block.scalar
            def _(eng):
                for f in self.q['act']:
                    f(eng)

            @block.vector
            def _(eng):
                for f in self.q['dve']:
                    f(eng)

            @block.gpsimd
            def _(eng):
                for f in self.q['pool']:
                    f(eng)

            @block.sync
            def _(eng):
                for f in self.q['sp']:
                    f(eng)


class Mem:
    def __init__(self, p, nc, nbytes):
        self.p = p
        self.nbytes = nbytes
        self.t = nc.alloc_sbuf_tensor('arena', [128, nbytes // 4], F32)

    def view(self, name, off, dtype, shape, parts=128):
        n = 1
        for s in shape:
            n *= s
        nb = n * _dsize(dtype)
        assert off % 4 == 0 and off + nb <= self.nbytes, (name, off, nb, self.nbytes)
        self.p.region(name, 'sb', off, nb)
        es = _dsize(dtype)
        ap = self.t.bitcast(dtype)[0:parts, off // es: off // es + n]
        if len(shape) == 2:
            ap = ap.rearrange("p (a b) -> p a b", a=shape[0])
        elif len(shape) == 3:
            ap = ap.rearrange("p (a b c) -> p a b c", a=shape[0], b=shape[1])
        elif len(shape) == 4:
            ap = ap.rearrange("p (a b c d) -> p a b c d", a=shape[0], b=shape[1], c=shape[2])
        return ap


def build_program():
    nc = bass.Bass("TRN2", target_bir_lowering=False)
    p = Prog(nc)

    def din(name, shape, dt=F32):
        return nc.dram_tensor(name, list(shape), dt, kind="ExternalInput").ap()

    x_d = din("x", [S, D])
    out_d = nc.dram_tensor("out", [S, D], F32, kind="ExternalOutput").ap()
    cT_d = din("cT", [128, NCH])
    pos_d = din("pos", [128, S], I32)
    wada_d = din("w_ada", [D, 9 * D])
    bada_d = din("b_adaT", [128, 72])
    gains_d = din("gainsT", [128, 3 * NCH])
    f1w1_d = din("f1w1", [D, DFF]); f1w3_d = din("f1w3", [D, DFF]); f1w2_d = din("f1w2", [DFF, D])
    f2w1_d = din("f2w1", [D, DFF]); f2w3_d = din("f2w3", [D, DFF]); f2w2_d = din("f2w2", [DFF, D])
    win_d = din("w_in", [D, INW])
    ident_d = din("ident", [128, 128])
    ones_d = din("ones", [128, 128])
    blk_d = din("blk", [128, 128]); rmat_d = din("rmat", [128, 128])
    inv_d = din("inv", [128, 2]); sgn_d = din("sgn", [128, 1])
    etab_d = din("etab", [128, 640]); diag2_d = din("diag2", [128, 640])
    gqk_d = din("gqk", [128, 2]); lqk_d = din("lqk", [128, 256])
    gsub_d = din("gsub", [128, 128]); gret_d = din("gret", [128, 128])
    dec_d = din("dec", [128, 8]); bm_d = din("bmT", [128, 16])
    wba_d = din("w_ba", [512, D]); wbr_d = din("w_br", [512, D]); wout_d = din("w_out", [D, D])

    mem = Mem(p, nc, 206 * 1024)
    pp = [nc.alloc_psum_tensor('pp%d' % i, [128, 1024], F32) for i in range(4)]

    class PB:
        def __init__(self, t, base, width):
            self.t, self.base, self.width = t, base, width

        def __getitem__(self, idx):
            rows, cols = idx
            a = cols.start or 0
            b = self.width if cols.stop is None else cols.stop
            assert cols.step is None and 0 <= a < b <= self.width
            return self.t[rows, self.base + a:self.base + b]

        def bitcast(self, dt):
            assert dt == BF16
            return PB(self.t.bitcast(dt), self.base * 2, self.width * 2)

    ps = [PB(pp[i // 2], (i % 2) * 512, 512) for i in range(8)]
    for i in range(8):
        p.region('ps%d' % i, 'ps', i, 1)

    OFF_XT = 0
    OFF_CONST = 64 * 1024
    OFF_FREE = 70 * 1024
    XT = mem.view('XT', OFF_XT, F32, [NCH, S])
    co = OFF_CONST
    IDF = mem.view('IDF', co, F32, [128]); co += 512
    ONESF = mem.view('ONESF', co, F32, [128]); co += 512
    IDB = mem.view('IDB', co, BF16, [128]); co += 256
    MOD = mem.view('MOD', co, F32, [72]); co += 288
    BADA = mem.view('BADA', co, F32, [72]); co += 288
    GAINS = mem.view('GAINS', co, F32, [24]); co += 96
    CT = mem.view('CT', co, F32, [NCH]); co += 32
    SCB = mem.view('SCB', co, BF16, [NCH]); co += 16 + 16
    NA = mem.view('NA', co, F32, [24]); co += 96
    HG = mem.view('HG', co, F32, [24]); co += 96
    EPSC = mem.view('EPSC', co, F32, [1]); co += 4
    BLKF = mem.view('BLKF', co, F32, [128]); co += 512
    RMB = mem.view('RMB', co, BF16, [128]); co += 256
    INV = mem.view('INV', co, F32, [2]); co += 8
    SGN = mem.view('SGN', co, F32, [1]); co += 4
    GQK = mem.view('GQK', co, F32, [2]); co += 8
    GSUB = mem.view('GSUB', co, F32, [128]); co += 512
    GRET = mem.view('GRET', co, F32, [128]); co += 512
    DEC = mem.view('DEC', co, F32, [8]); co += 32
    BM = mem.view('BM', co, F32, [16]); co += 64
    LGN = mem.view('LGN', co, F32, [8]); co += 32
    LG = mem.view('LG', co, F32, [8]); co += 32
    SCT = mem.view('SCT', co, F32, [8, 14]); co += 448
    NLAM = mem.view('NLAM', co, F32, [1]); co += 4
    LTMP = mem.view('LTMP', co, F32, [8]); co += 32
    assert co <= OFF_FREE, co

    p.dma('sp', [(IDF, ident_d), (ONESF, ones_d), (BADA, bada_d), (GAINS, gains_d), (CT, cT_d)],
          writes=['IDF', 'ONESF', 'BADA', 'GAINS', 'CT'], sem='D_const')
    p.dma('pool', [(IDB, ident_d), (RMB, rmat_d)], writes=['IDB', 'RMB'], sem='D_constb')
    p.dma('sp', [(BLKF, blk_d), (INV, inv_d), (SGN, sgn_d), (GQK, gqk_d), (GSUB, gsub_d), (GRET, gret_d),
                 (DEC, dec_d), (BM, bm_d)],
          writes=['BLKF', 'INV', 'SGN', 'GQK', 'GSUB', 'GRET', 'DEC', 'BM'], sem='D_const2')
    p.op('dve', lambda e: e.memset(EPSC, EPS), writes=['EPSC'])

    p.op('act', lambda e: e.activation(out=SCB, in_=CT, func=AF.Silu), reads=['CT'], writes=['SCB'])
    WA_OFF = OFF_FREE + 40 * 1024
    wa = [mem.view('WA%d' % i, WA_OFF + i * 16384, BF16, [NCH, 1024]) for i in range(2)]
    wada_v = wada_d.rearrange("(c p) n -> p c n", p=128)
    for g in range(9):
        sl = g % 2
        key = 'WA%d' % sl
        p.dma('pool', [(wa[sl][:, c4 * 2:(c4 + 1) * 2, :], wada_v[:, c4 * 2:(c4 + 1) * 2, g * 1024:(g + 1) * 1024])
                       for c4 in range(4)], writes=[key])
        for j in range(8):
            col = g * 8 + j
            for k in range(NCH):
                p.op('pe', lambda e, sl=sl, j=j, k=k, col=col: e.matmul(
                    ps[7][:, col:col + 1], lhsT=wa[sl][:, k, j * 128:(j + 1) * 128], rhs=SCB[:, k:k + 1],
                    start=(k == 0), stop=(k == NCH - 1)),
                    reads=[key, 'SCB'], writes=['ps7'], mark=(k == NCH - 1 and j == 7))
    p.op('dve', lambda e: e.tensor_tensor(out=MOD, in0=ps[7][:, 0:72], in1=BADA, op=ALU.add),
         reads=['ps7', 'BADA'], writes=['MOD'])
    for n in range(3):
        sc = MOD[:, (3 * n + 1) * 8:(3 * n + 2) * 8]
        p.op('dve', lambda e, n=n, sc=sc: e.scalar_tensor_tensor(
            out=NA[:, n * 8:(n + 1) * 8], in0=sc, scalar=1.0, in1=GAINS[:, n * 8:(n + 1) * 8],
            op0=ALU.add, op1=ALU.mult), reads=['MOD', 'GAINS'], writes=[('NA', n)])
        gt = MOD[:, (3 * n + 2) * 8:(3 * n + 3) * 8]
        p.op('dve', lambda e, n=n, gt=gt: e.tensor_scalar(
            out=HG[:, n * 8:(n + 1) * 8], in0=gt, scalar1=(1.0 if n == 1 else 0.5), scalar2=None,
            op0=ALU.mult), reads=['MOD'], writes=[('HG', n)])

    XS_OFF = OFF_FREE
    xs = [mem.view('XS%d' % i, XS_OFF + i * 16384, F32, [4, D]) for i in range(2)]
    x_v = x_d.rearrange("(t p) d -> p t d", p=128)
    for g in range(4):
        sl = g % 2
        key = 'XS%d' % sl
        p.dma('sp', [(xs[sl][:, t:t + 1, :], x_v[:, g * 4 + t:g * 4 + t + 1, :]) for t in range(4)], writes=[key])
        for c in range(NCH):
            pb = (g * NCH + c) % 2
            pk = 'ps%d' % pb
            for t in range(4):
                p.op('pe', lambda e, sl=sl, t=t, c=c, pb=pb: e.transpose(
                    ps[pb][:, t * 128:(t + 1) * 128], xs[sl][:, t, c * 128:(c + 1) * 128], IDF),
                    reads=[key, 'IDF'], writes=[pk], mark=(t == 3))
            eng = 'dve' if c % 2 == 0 else 'act'
            dst = XT[:, c, g * 512:(g + 1) * 512]
            if eng == 'dve':
                p.op('dve', lambda e, pb=pb, dst=dst: e.tensor_copy(out=dst, in_=ps[pb][:, :]),
                     reads=[pk], writes=[('XT', c, g)])
            else:
                p.op('act', lambda e, pb=pb, dst=dst: e.copy(out=dst, in_=ps[pb][:, :]),
                     reads=[pk], writes=[('XT', c, g)])


    O_HTG = OFF_FREE
    O_GT = OFF_FREE + 16 * 1024
    O_W13 = OFF_FREE + 60 * 1024
    O_W2 = OFF_FREE + 84 * 1024
    O_SCR = OFF_FREE + 106 * 1024

    def norm_scratch(tagp, ntok, scr_off):
        SQ = mem.view(tagp + 'SQ', scr_off, F32, [ntok])
        RS = mem.view(tagp + 'RS', scr_off + 4 * ntok, F32, [ntok])
        TM = [mem.view(tagp + 'TM%d' % i, scr_off + 8 * ntok + i * 4 * ntok, F32, [ntok]) for i in range(2)]
        return SQ, RS, TM

    def rms_norm_tokens(n, tok0, ntok, HT_dst, tagp, scr, hkeyf, add_eng='dve'):
        SQ, RS, TM = scr
        g0 = tok0 // 512
        ng = ntok // 512
        for c in range(NCH):
            xk = [('XT', c, g0 + i) for i in range(ng)]
            src = XT[:, c, tok0:tok0 + ntok]
            if c == 0:
                p.op('act', lambda e, src=src: e.activation(out=SQ, in_=src, func=AF.Square),
                     reads=xk, writes=[tagp + 'SQ'])
            else:
                t = TM[c % 2]
                tk = tagp + 'TM%d' % (c % 2)
                p.op('act', lambda e, src=src, t=t: e.activation(out=t, in_=src, func=AF.Square),
                     reads=xk, writes=[tk])
                p.op(add_eng, lambda e, t=t: e.tensor_tensor(out=SQ, in0=SQ, in1=t, op=ALU.add),
                     reads=[tk, tagp + 'SQ'], writes=[tagp + 'SQ'])
        for i in range(ng):
            pb = 6 + (i % 2)
            pk = 'ps%d' % pb
            p.op('pe', lambda e, i=i, pb=pb: e.matmul(ps[pb][:, :], lhsT=ONESF, rhs=SQ[:, i * 512:(i + 1) * 512],
                                                     start=True, stop=True),
                 reads=[tagp + 'SQ', 'ONESF'], writes=[pk])
            t = TM[i % 2]
            tk = tagp + 'TM%d' % (i % 2)
            p.op('act', lambda e, pb=pb, t=t: e.activation(out=t[:, 0:512], in_=ps[pb][:, :], func=AF.Ln,
                                                           scale=1.0 / D, bias=EPSC),
                 reads=[pk, 'EPSC'], writes=[tk])
            p.op('act', lambda e, i=i, t=t: e.activation(out=RS[:, i * 512:(i + 1) * 512], in_=t[:, 0:512],
                                                         func=AF.Exp, scale=-0.5),
                 reads=[tk], writes=[(tagp + 'RS', i)])
        for c in range(NCH):
            xk = [('XT', c, g0 + i) for i in range(ng)]
            t = TM[c % 2]
            tk = tagp + 'TM%d' % (c % 2)
            p.op('dve', lambda e, c=c, t=t: e.tensor_tensor(out=t, in0=XT[:, c, tok0:tok0 + ntok], in1=RS, op=ALU.mult),
                 reads=xk + [(tagp + 'RS', i) for i in range(ng)], writes=[tk])
            p.op('act', lambda e, c=c, t=t: e.activation(out=HT_dst[:, c, 0:ntok], in_=t, func=AF.Identity,
                                                         scale=NA[:, n * 8 + c:n * 8 + c + 1],
                                                         bias=MOD[:, 3 * n * 8 + c:3 * n * 8 + c + 1]),
                 reads=[tk, ('NA', n), 'MOD'], writes=[hkeyf(c)])

    HT_dst_key = ['HTG']

    def ffn(n, w1_d, w3_d, w2_d):
        TG = 1024
        U = 'f%d_' % n
        scr = norm_scratch(U, TG, O_SCR)
        HTG = mem.view(U + 'HTG', O_HTG, BF16, [NCH, TG])
        GT = mem.view(U + 'GT', O_GT, BF16, [NFC, TG])
        W1s = [mem.view(U + 'W1s%d' % i, O_W13 + i * 8192, BF16, [NCH, 256]) for i in range(3)]
        W3s = [mem.view(U + 'W3s%d' % i, O_W13 + i * 8192 + 4096, BF16, [NCH, 256]) for i in range(3)]
        W2s = [mem.view(U + 'W2s%d' % i, O_W2 + i * 11264, BF16, [NFC, 256]) for i in range(2)]
        SL = [mem.view(U + 'SL%d' % i, O_SCR + 16384 + i * 2048, F32, [512]) for i in range(2)]
        w1_v = w1_d.rearrange("(c p) n -> p c n", p=128)
        w3_v = w3_d.rearrange("(c p) n -> p c n", p=128)
        w2_v = w2_d.rearrange("(f p) n -> p f n", p=128)
        loads = []
        for tg in range(2):
            for j in range(11):
                loads.append(('a', j))
            for j in range(4):
                loads.append(('b', j))
        state = {'nxt': 0}

        def prefetch(i):
            while state['nxt'] < len(loads):
                kind, j = loads[state['nxt']]
                if state['nxt'] > i + (2 if kind == 'a' else 1):
                    break
                if kind == 'a':
                    sl = j % 3
                    p.dma('pool', [(W1s[sl][:, 0:4, :], w1_v[:, 0:4, j * 256:(j + 1) * 256]),
                                   (W1s[sl][:, 4:8, :], w1_v[:, 4:8, j * 256:(j + 1) * 256]),
                                   (W3s[sl][:, 0:4, :], w3_v[:, 0:4, j * 256:(j + 1) * 256]),
                                   (W3s[sl][:, 4:8, :], w3_v[:, 4:8, j * 256:(j + 1) * 256])],
                          writes=[U + 'W1s%d' % sl, U + 'W3s%d' % sl], sem='D_W13_%d' % sl)
                else:
                    sl = j % 2
                    p.dma('pool', [(W2s[sl][:, a:b, :], w2_v[:, a:b, j * 256:(j + 1) * 256])
                                   for (a, b) in ((0, 6), (6, 11), (11, 17), (17, 22))],
                          writes=[U + 'W2s%d' % sl], sem='D_W2_%d' % sl)
                state['nxt'] += 1

        li = 0
        for tg in range(2):
            tok0 = tg * TG
            prefetch(li)
            HT_dst_key[0] = U + 'HTG'
            rms_norm_tokens(n, tok0, TG, HTG, U, scr, lambda c: (U + 'HTG', c))
            for j in range(11):
                prefetch(li)
                sl = j % 3
                for fl in range(2):
                    f = j * 2 + fl
                    for half in range(2):
                        q = (f * 2 + half) % 2
                        A, Bk = ps[2 * q], ps[2 * q + 1]
                        ak, bk = 'ps%d' % (2 * q), 'ps%d' % (2 * q + 1)
                        for k in range(NCH):
                            p.op('pe', lambda e, A=A, sl=sl, k=k, fl=fl, half=half: e.matmul(
                                A[:, :], lhsT=W1s[sl][:, k, fl * 128:(fl + 1) * 128],
                                rhs=HTG[:, k, half * 512:(half + 1) * 512], start=(k == 0), stop=(k == NCH - 1)),
                                reads=[U + 'W1s%d' % sl, (U + 'HTG', k)], writes=[ak], mark=(k == NCH - 1))
                        for k in range(NCH):
                            p.op('pe', lambda e, Bk=Bk, sl=sl, k=k, fl=fl, half=half: e.matmul(
                                Bk[:, :], lhsT=W3s[sl][:, k, fl * 128:(fl + 1) * 128],
                                rhs=HTG[:, k, half * 512:(half + 1) * 512], start=(k == 0), stop=(k == NCH - 1)),
                                reads=[U + 'W3s%d' % sl, (U + 'HTG', k)], writes=[bk], mark=(k == NCH - 1))
                        p.op('act', lambda e, A=A, q=q: e.activation(out=SL[q], in_=A[:, :], func=AF.Silu),
                             reads=[ak], writes=[U + 'SL%d' % q])
                        p.op('dve', lambda e, Bk=Bk, q=q, f=f, half=half: e.tensor_tensor(
                            out=GT[:, f, half * 512:(half + 1) * 512], in0=SL[q], in1=Bk[:, :], op=ALU.mult),
                            reads=[U + 'SL%d' % q, bk], writes=[(U + 'GT', f, half)])
                li += 1
            for j in range(4):
                prefetch(li)
                sl = j % 2
                for dl in range(2):
                    d = j * 2 + dl
                    for half in range(2):
                        pb = 4 + (d * 2 + half) % 2
                        ck = 'ps%d' % pb
                        for f in range(NFC):
                            p.op('pe', lambda e, pb=pb, sl=sl, f=f, dl=dl, half=half: e.matmul(
                                ps[pb][:, :], lhsT=W2s[sl][:, f, dl * 128:(dl + 1) * 128],
                                rhs=GT[:, f, half * 512:(half + 1) * 512], start=(f == 0), stop=(f == NFC - 1)),
                                reads=[U + 'W2s%d' % sl, (U + 'GT', f, half)], writes=[ck], mark=(f == NFC - 1))
                        g = tg * 2 + half
                        xs_ = XT[:, d, g * 512:(g + 1) * 512]
                        p.op('dve', lambda e, pb=pb, d=d, xs_=xs_: e.scalar_tensor_tensor(
                            out=xs_, in0=ps[pb][:, :], scalar=HG[:, n * 8 + d:n * 8 + d + 1], in1=xs_,
                            op0=ALU.mult, op1=ALU.add),
                            reads=[ck, ('HG', n), ('XT', d, g)], writes=[('XT', d, g)])
                li += 1

    if STAGE >= 1:
        ffn(0, f1w1_d, f1w3_d, f1w2_d)


    PI_LO = 3.1415925
    TWO_PI = 2.0 * math.pi
    CW1 = 6.28125
    CW2 = TWO_PI - 6.28125

    def mixer():
        n = 1
        M0 = OFF_FREE
        O_HT = M0
        O_RT = M0 + 32 * 1024
        O_RQ = M0 + 48 * 1024
        O_OA = O_RQ + 25088
        O_Y = O_OA + 16384
        O_W = O_Y + 16384
        O_SC = O_W + 12288
        HT = mem.view('HT', O_HT, BF16, [NCH, S])
        OAT = mem.view('OAT', O_OA, BF16, [4, S])
        WS = [mem.view('WS%d' % i, O_W + i * 4096, BF16, [NCH, 256]) for i in range(3)]
        win_v = win_d.rearrange("(c p) n -> p c n", p=128)
        wst = {'i': 0}

        def wload(col0, ncols):
            sl = wst['i'] % 3
            wst['i'] += 1
            p.dma('pool', [(WS[sl][:, 0:4, 0:ncols], win_v[:, 0:4, col0:col0 + ncols]),
                           (WS[sl][:, 4:8, 0:ncols], win_v[:, 4:8, col0:col0 + ncols])],
                  writes=['WS%d' % sl], sem='D_WS%d' % sl)
            return sl

        nscr = norm_scratch('mx_', 512, O_SC)
        for g in range(4):
            rms_norm_tokens(1, g * 512, 512, HT[:, :, g * 512:(g + 1) * 512], 'mx_', nscr,
                            lambda c, g=g: ('HT', c, g), add_eng='pool')
        htk = lambda k, g: ('HT', k, g)

        LQK = mem.view('LQK', O_RQ, F32, [256])
        p.dma('sp', [(LQK, lqk_d)], writes=['LQK'], sem='D_LQK')
        PRD = mem.view('PRD', O_RQ + 1024, F32, [128])
        p.op('dve', lambda e: e.tensor_tensor(out=PRD[:, 0:64], in0=LQK[:, 0:64], in1=LQK[:, 64:128], op=ALU.mult),
             reads=['LQK'], writes=[('PRD', 0)])
        p.op('dve', lambda e: e.tensor_tensor(out=PRD[:, 64:128], in0=LQK[:, 128:192], in1=LQK[:, 192:256], op=ALU.mult),
             reads=['LQK'], writes=[('PRD', 1)])
        p.op('dve', lambda e: e.tensor_reduce(out=LTMP[:, 0:2], in_=PRD.rearrange("p (a b) -> p a b", a=2),
                                              axis=AX.X, op=ALU.add),
             reads=[('PRD', 0), ('PRD', 1)], writes=[('LTMP', 0)])
        p.op('act', lambda e: e.activation(out=LTMP[:, 2:4], in_=LTMP[:, 0:2], func=AF.Exp),
             reads=[('LTMP', 0)], writes=[('LTMP', 1)])
        p.op('dve', lambda e: e.tensor_tensor(out=LTMP[:, 4:5], in0=LTMP[:, 3:4], in1=LTMP[:, 2:3], op=ALU.subtract),
             reads=[('LTMP', 1)], writes=[('LTMP', 2)])
        p.op('dve', lambda e: e.tensor_scalar(out=NLAM, in0=LTMP[:, 4:5], scalar1=-LAM_INIT, scalar2=None, op0=ALU.add),
             reads=[('LTMP', 2)], writes=['NLAM'])
        p.op('dve', lambda e: e.tensor_scalar(out=GSUB, in0=GSUB, scalar1=1.0 - LAM_INIT, scalar2=None, op0=ALU.mult),
             reads=['GSUB'], writes=['GSUB'])
        p.op('act', lambda e: e.activation(out=LGN, in_=DEC, func=AF.Exp, scale=-1.0), reads=['DEC'], writes=['LGN'])
        p.op('act', lambda e: e.activation(out=LGN, in_=LGN, func=AF.Ln, bias=1.0), reads=['LGN'], writes=['LGN'])
        p.op('dve', lambda e: e.tensor_scalar(out=LG, in0=LGN, scalar1=-1.0, scalar2=None, op0=ALU.mult),
             reads=['LGN'], writes=['LG'])
        for m in range(14):
            p.op('act', lambda e, m=m: e.activation(out=SCT[:, :, m], in_=LGN, func=AF.Exp, scale=-128.0 * m),
                 reads=['LGN'], writes=[('SCT', m)])
        sct_keys = [('SCT', m) for m in range(14)]

        def make_tables(br, U):
            POSI = mem.view(U + 'POSI', O_RQ, I32, [S])
            POSF = mem.view(U + 'POSF', O_RQ + 8192, F32, [S])
            KI = mem.view(U + 'KI', O_RQ + 16384, I32, [S])
            M2 = mem.view(U + 'M2', O_RQ + 16384, F32, [S])
            COS = mem.view(U + 'COS', O_RT, F32, [S])
            SIN = mem.view(U + 'SIN', O_RT + 8192, F32, [S])
            kP, kF, kK, kM, kC, kS = U + 'POSI', U + 'POSF', U + 'KI', U + 'M2', U + 'COS', U + 'SIN'
            p.dma('sp', [(POSI, pos_d)], writes=[kP], sem='D_POSI')
            p.op('dve', lambda e: e.tensor_copy(out=POSF, in_=POSI), reads=[kP], writes=[kF])
            p.op('dve', lambda e: e.tensor_scalar(out=COS, in0=POSF, scalar1=INV[:, br:br + 1], scalar2=None,
                                                  op0=ALU.mult), reads=[kF, 'INV'], writes=[kC])
            p.op('dve', lambda e: e.tensor_scalar(out=KI, in0=COS, scalar1=1.0 / TWO_PI, scalar2=None, op0=ALU.mult),
                 reads=[kC], writes=[kK])
            p.op('dve', lambda e: e.tensor_copy(out=POSF, in_=KI), reads=[kK], writes=[kF])
            p.op('dve', lambda e: e.scalar_tensor_tensor(out=COS, in0=POSF, scalar=-CW1, in1=COS, op0=ALU.mult,
                                                         op1=ALU.add), reads=[kF, kC], writes=[kC])
            p.op('dve', lambda e: e.scalar_tensor_tensor(out=COS, in0=POSF, scalar=-CW2, in1=COS, op0=ALU.mult,
                                                         op1=ALU.add), reads=[kF, kC], writes=[kC])

            def wrap(dst, kd, srcap, ksrc, shift):
                p.op('dve', lambda e: e.tensor_scalar(out=dst, in0=srcap, scalar1=shift, scalar2=None, op0=ALU.add),
                     reads=[ksrc], writes=[kd])
                p.op('dve', lambda e: e.tensor_scalar(out=POSF, in0=dst, scalar1=PI_LO, scalar2=None, op0=ALU.is_gt),
                     reads=[kd], writes=[kF])
                p.op('dve', lambda e: e.tensor_scalar(out=M2, in0=dst, scalar1=-PI_LO, scalar2=None, op0=ALU.is_lt),
                     reads=[kd], writes=[kM])
                p.op('dve', lambda e: e.scalar_tensor_tensor(out=dst, in0=POSF, scalar=-TWO_PI, in1=dst, op0=ALU.mult,
                                                             op1=ALU.add), reads=[kF, kd], writes=[kd])
                p.op('dve', lambda e: e.scalar_tensor_tensor(out=dst, in0=M2, scalar=TWO_PI, in1=dst, op0=ALU.mult,
                                                             op1=ALU.add), reads=[kM, kd], writes=[kd])
                p.op('dve', lambda e: e.tensor_scalar(out=dst, in0=dst, scalar1=PI_LO, scalar2=-PI_LO, op0=ALU.min,
                                                      op1=ALU.max), reads=[kd], writes=[kd])
            wrap(SIN, kS, COS, kC, 0.0)
            wrap(COS, kC, COS, kC, math.pi / 2)
            p.op('act', lambda e: e.activation(out=SIN, in_=SIN, func=AF.Sin, scale=SGN[:, 0:1]),
                 reads=[kS, 'SGN'], writes=[kS])
            p.op('act', lambda e: e.activation(out=COS, in_=COS, func=AF.Sin), reads=[kC], writes=[kC])
            return COS, SIN, kC, kS

        uq = {'u': 0}

        def qk_chunk(U, sl, wc0, dsts, dkeyf, do_norm, gain_col, tabs, scr):
            COS, SIN, kC, kS = tabs
            SQs, RSs, QN, T1s = scr
            for g in range(4):
                u = uq['u']; uq['u'] += 1
                b = u % 2
                SQ, RS, T1 = SQs[b], RSs[b], T1s[b]
                kq, kr, kt1 = U + 'SQ%d' % b, U + 'RS%d' % b, U + 'T1%d' % b
                zk, sk, rk = 'ps%d' % b, 'ps%d' % (2 + b), 'ps%d' % (4 + b)
                Z, SSp, ROT = ps[b], ps[2 + b], ps[4 + b]
                tk = slice(g * 512, (g + 1) * 512)
                for k in range(NCH):
                    p.op('pe', lambda e, Z=Z, k=k, tk=tk: e.matmul(Z[:, :], lhsT=WS[sl][:, k, wc0:wc0 + 128],
                                                                   rhs=HT[:, k, tk], start=(k == 0), stop=(k == NCH - 1)),
                         reads=['WS%d' % sl, htk(k, g)], writes=[zk], mark=(k == NCH - 1))
                qn = QN[:, b, :]
                qk_ = (U + 'QN', b)
                if do_norm:
                    p.op('act', lambda e, Z=Z, SQ=SQ: e.activation(out=SQ, in_=Z[:, :], func=AF.Square),
                         reads=[zk], writes=[kq])
                    p.op('pe', lambda e, SSp=SSp, SQ=SQ: e.matmul(SSp[:, :], lhsT=BLKF, rhs=SQ, start=True, stop=True),
                         reads=[kq, 'BLKF'], writes=[sk])
                    p.op('act', lambda e, SSp=SSp, RS=RS: e.activation(out=RS, in_=SSp[:, :], func=AF.Ln, scale=1.0 / 64,
                                                                        bias=EPSC), reads=[sk, 'EPSC'], writes=[kr])
                    p.op('act', lambda e, RS=RS: e.activation(out=RS, in_=RS, func=AF.Exp, scale=-0.5),
                         reads=[kr], writes=[kr])
                    p.op('dve', lambda e, Z=Z, qn=qn, RS=RS: e.scalar_tensor_tensor(
                        out=qn, in0=Z[:, :], scalar=GQK[:, gain_col:gain_col + 1], in1=RS, op0=ALU.mult, op1=ALU.mult),
                        reads=[zk, 'GQK', kr], writes=[qk_])
                else:
                    p.op('act', lambda e, Z=Z, qn=qn: e.copy(out=qn, in_=Z[:, :]), reads=[zk], writes=[qk_])
                p.op('pe', lambda e, ROT=ROT, qn=qn: e.matmul(ROT[:, :], lhsT=RMB, rhs=qn, start=True, stop=True),
                     reads=[qk_, 'RMB'], writes=[rk])
                p.op('dve', lambda e, qn=qn, tk=tk, T1=T1: e.tensor_tensor(out=T1, in0=qn, in1=COS[:, tk], op=ALU.mult),
                     reads=[qk_, kC], writes=[kt1])
                p.op('dve', lambda e, ROT=ROT, tk=tk, SQ=SQ: e.tensor_tensor(out=SQ, in0=ROT[:, :], in1=SIN[:, tk], op=ALU.mult),
                     reads=[rk, kS], writes=[kq])
                for (p0, p1, dst) in dsts:
                    p.op('dve', lambda e, tk=tk, T1=T1, SQ=SQ, p0=p0, p1=p1, dst=dst: e.tensor_tensor(
                        out=dst[:, tk], in0=T1[p0:p1, :], in1=SQ[p0:p1, :], op=ALU.add),
                        reads=[kt1, kq], writes=[dkeyf(g)])

        def qk_scratch(U):
            SQs = [mem.view(U + 'SQ%d' % i, O_SC + i * 8192, F32, [512]) for i in range(2)]
            RSs = [mem.view(U + 'RS%d' % i, O_SC + i * 8192 + 2048, F32, [512]) for i in range(2)]
            QN = mem.view(U + 'QN', O_SC + 4096, BF16, [2, 512])
            T1s = [mem.view(U + 'T1%d' % i, O_SC + 6144 + i * 6144, F32, [512]) for i in range(2)]
            return SQs, RSs, QN, T1s

        pend = []

        def flush():
            for f in pend:
                f()
            del pend[:]

        ps_bf = [ps[6].bitcast(BF16), ps[7].bitcast(BF16)]
        tcount = {'i': 0}

        def defer_transpose(src_ap, src_key, dst_ap, dst_key):
            def f():
                i = tcount['i']; tcount['i'] += 1
                b = i % 2
                pk = 'ps%d' % (6 + b)
                p.op('pe', lambda e: e.transpose(ps_bf[b][:, 0:128], src_ap, IDB), reads=[src_key, 'IDB'], writes=[pk])
                if i % 2 == 0:
                    p.op('act', lambda e: e.copy(out=dst_ap, in_=ps_bf[b][:, 0:128]), reads=[pk], writes=[dst_key])
                else:
                    p.op('dve', lambda e: e.tensor_copy(out=dst_ap, in_=ps_bf[b][:, 0:128]), reads=[pk], writes=[dst_key])
            pend.append(f)

        tabs = make_tables(0, 'da_')
        for pr in range(2):
            U = 'da%d_' % pr
            QZ = mem.view(U + 'QZ', O_Y, BF16, [2, 2, S])
            p.op('dve', lambda e, QZ=QZ: e.memset(QZ, 0.0), writes=[(U + 'QZ', hl_, g_) for hl_ in range(2) for g_ in range(4)])
            KT = mem.view(U + 'KT', O_RQ + 8192, BF16, [2, S])
            VA = mem.view(U + 'VA', O_RQ + 16384, BF16, [NT, 2, 130])
            scr = qk_scratch(U)
            slq = wload(0 + pr * 256, 256)
            for hl in range(2):
                qk_chunk(U, slq, hl * 128, [(0, 64, QZ[0:64, hl, 0, :]), (64, 128, QZ[64:128, hl, 1, :])],
                         lambda g, hl=hl: (U + 'QZ', hl, g), True, 0, tabs, scr)
            slk = wload(512 + pr * 256, 256)
            for hl in range(2):
                qk_chunk(U, slk, hl * 128, [(0, 128, KT[:, hl, :])], lambda g, hl=hl: (U + 'KT', hl, g), True, 1, tabs, scr)
            slv = wload(1024 + pr * 256, 256)
            p.op('dve', lambda e, VA=VA: e.memset(VA[:, :, :, 128:130], 1.0), writes=[(U + 'VA', 'ones')])
            for tt in range(NT):
                b = tt % 2
                pk = 'ps%d' % (6 + b)
                for k in range(NCH):
                    p.op('pe', lambda e, b=b, k=k, tt=tt, slv=slv: e.matmul(ps[6 + b][:, 0:256], lhsT=HT[:, k, tt * 128:(tt + 1) * 128],
                                                                 rhs=WS[slv][:, k, 0:256], start=(k == 0), stop=(k == NCH - 1)),
                         reads=['WS%d' % slv, htk(k, tt // 4)], writes=[pk], mark=(k == NCH - 1))
                srcv = ps[6 + b][:, 0:256].rearrange("p (h d) -> p h d", h=2)
                if tt % 2 == 0:
                    p.op('act', lambda e, srcv=srcv, tt=tt, VA=VA: e.copy(out=VA[:, tt, :, 0:128], in_=srcv),
                         reads=[pk], writes=[(U + 'VA', tt)])
                else:
                    p.op('dve', lambda e, srcv=srcv, tt=tt, VA=VA: e.tensor_copy(out=VA[:, tt, :, 0:128], in_=srcv),
                         reads=[pk], writes=[(U + 'VA', tt)])
            PT = mem.view(U + 'PT', O_SC + 8192, BF16, [4, 2, 256])
            EO = O_SC + 15360
            RC = mem.view(U + 'RC', EO, F32, [2, 4])
            T1e = mem.view(U + 'T1e', EO + 64, F32, [2, 128])
            Oe = mem.view(U + 'Oe', EO + 64 + 1024, F32, [2, 128])
            JK = mem.view(U + 'JK', EO + 64 + 2048, F32, [128])
            ONe = mem.view(U + 'ONe', EO + 64 + 2560, BF16, [2, 128])
            def da_qk(i, hl, qg, kt, QZ=QZ, KT=KT, PT=PT, U=U):
                b = i % 4
                sb = i % 4
                pk = 'ps%d' % sb
                for m in range(2):
                    p.op('pe', lambda e, sb=sb, m=m, kt=kt, qg=qg, hl=hl, QZ=QZ, KT=KT: e.matmul(
                        ps[sb][:, m * 256:(m + 1) * 256], lhsT=KT[:, hl, kt * 128:(kt + 1) * 128],
                        rhs=QZ[:, hl, m, qg * 256:(qg + 1) * 256], start=True, stop=True, skip_group_check=True),
                        reads=[(U + 'KT', hl, kt // 4), (U + 'QZ', hl, qg // 2)], writes=[pk], mark=(m == 1))
                p.op('act', lambda e, b=b, sb=sb, PT=PT: e.activation(
                    out=PT[:, b, :, :], in_=ps[sb][:, 0:512].rearrange("p (m c) -> p m c", m=2), func=AF.Exp, scale=0.125),
                    reads=[pk], writes=[(U + 'PT', b, 0), (U + 'PT', b, 1)])

            ACS = mem.view(U + 'ACS', O_SC, F32, [2, 260])

            def da_pv(i, hl, qg, kt, pr=pr, KT=KT, PT=PT, VA=VA, RC=RC, T1e=T1e, Oe=Oe, JK=JK, ONe=ONe, U=U, ACS=ACS):
                b = i % 4
                hd = pr * 2 + hl
                if kt == 4:
                    flush()
                for qt in range(2):
                    for m in range(2):
                        first = (kt == 0 and m == 0)
                        p.op('pe', lambda e, qt=qt, m=m, b=b, kt=kt, hl=hl, first=first, PT=PT, VA=VA: e.matmul(
                            ps[4 + qt][:, m * 130:m * 130 + 129], lhsT=PT[:, b, m, qt * 128:(qt + 1) * 128],
                            rhs=VA[:, kt, hl, 0:129], start=first, stop=(kt == NT - 1), skip_group_check=True),
                            reads=[(U + 'PT', b, m), (U + 'VA', kt), (U + 'VA', 'ones')], writes=['ps%d' % (4 + qt)],
                            mark=(m == 1))
                if kt != NT - 1:
                    return
                for qt in range(2):
                    p.op('dve', lambda e, qt=qt, ACS=ACS: e.tensor_copy(out=ACS[:, qt, :], in_=ps[4 + qt][:, 0:260]),
                         reads=['ps%d' % (4 + qt)], writes=[(U + 'ACS', qt)])
                for qt in range(2):
                    ak = (U + 'ACS', qt)
                    acc = ACS[:, qt, :]
                    sums = acc.rearrange("p (m c) -> p m c", m=2)[:, :, 128:129]
                    p.op('dve', lambda e, qt=qt, sums=sums, RC=RC: e.reciprocal(out=RC[:, qt, 0:2].rearrange("p (m o) -> p m o", o=1), in_=sums),
                         reads=[ak], writes=[(U + 'RC', qt)])
                    p.op('dve', lambda e, qt=qt, RC=RC: e.tensor_scalar(out=RC[:, qt, 2:3], in0=RC[:, qt, 1:2], scalar1=NLAM[:, 0:1],
                                                                     scalar2=None, op0=ALU.mult),
                         reads=[(U + 'RC', qt), 'NLAM'], writes=[(U + 'RC', qt)])
                    p.op('dve', lambda e, qt=qt, acc=acc, RC=RC, T1e=T1e: e.tensor_scalar(
                        out=T1e[:, qt, :], in0=acc[:, 130:258], scalar1=RC[:, qt, 2:3], scalar2=None, op0=ALU.mult),
                        reads=[ak, (U + 'RC', qt)], writes=[(U + 'T1e', qt)])
                    p.op('dve', lambda e, qt=qt, acc=acc, RC=RC, T1e=T1e, Oe=Oe: e.scalar_tensor_tensor(
                        out=Oe[:, qt, :], in0=acc[:, 0:128], scalar=RC[:, qt, 0:1], in1=T1e[:, qt, :],
                        op0=ALU.mult, op1=ALU.add),
                        reads=[ak, (U + 'RC', qt), (U + 'T1e', qt)], writes=[(U + 'Oe', qt)])
                    p.op('dve', lambda e, qt=qt, Oe=Oe, JK=JK: e.tensor_tensor(out=JK, in0=Oe[:, qt, :], in1=Oe[:, qt, :], op=ALU.mult),
                         reads=[(U + 'Oe', qt)], writes=[U + 'JK'])
                    p.op('dve', lambda e, qt=qt, JK=JK, RC=RC: e.tensor_reduce(out=RC[:, qt, 3:4], in_=JK, axis=AX.X, op=ALU.add),
                         reads=[U + 'JK'], writes=[(U + 'RC', qt, 's')])

                def tail(qg=qg, hd=hd):
                    for qt in range(2):
                        p.op('act', lambda e, qt=qt, RC=RC: e.activation(out=RC[:, qt, 3:4], in_=RC[:, qt, 3:4], func=AF.Ln,
                                                                       scale=1.0 / 128, bias=EPSC),
                             reads=[(U + 'RC', qt, 's'), 'EPSC'], writes=[(U + 'RC', qt, 's')])
                        p.op('act', lambda e, qt=qt, RC=RC: e.activation(out=RC[:, qt, 3:4], in_=RC[:, qt, 3:4], func=AF.Exp,
                                                                       scale=-0.5),
                             reads=[(U + 'RC', qt, 's')], writes=[(U + 'RC', qt, 's')])
                    for qt in range(2):
                        p.op('dve', lambda e, qt=qt, Oe=Oe, RC=RC, ONe=ONe: e.scalar_tensor_tensor(
                            out=ONe[:, qt, :], in0=Oe[:, qt, :], scalar=RC[:, qt, 3:4], in1=GSUB, op0=ALU.mult, op1=ALU.mult),
                            reads=[(U + 'Oe', qt), (U + 'RC', qt, 's'), 'GSUB'], writes=[(U + 'ONe', qt)])
                pend.append(tail)
                for qt in range(2):
                    tq = qg * 2 + qt
                    defer_transpose(ONe[:, qt, :], (U + 'ONe', qt), OAT[:, hd, tq * 128:(tq + 1) * 128], ('OAT', hd, tq // 4))

            its = [(hl, qg, kt) for hl in range(2) for qg in range(8) for kt in range(NT)]
            da_qk(0, *its[0])
            da_qk(1, *its[1])
            for i, it in enumerate(its):
                if i + 2 < len(its):
                    da_qk(i + 2, *its[i + 2])
                da_pv(i, *it)
            flush()

        if STAGE == 2:
            return

        YT = mem.view('YT', O_Y, BF16, [4, S])
        tabs = make_tables(1, 'rt_')
        for pr in range(2):
            U = 'rt%d_' % pr
            QT = mem.view(U + 'QT', O_RQ, BF16, [S])
            KT = mem.view(U + 'KT', O_RQ + 4096, BF16, [S])
            VR = mem.view(U + 'VR', O_RQ + 8192, BF16, [NT, 256])
            SG = mem.view(U + 'SG', O_RQ + 16384, BF16, [NT, 256])
            scr = qk_scratch(U)
            slq = wload(1536 + pr * 128, 128)
            qk_chunk(U, slq, 0, [(0, 128, QT)], lambda g: (U + 'QT', g), False, 0, tabs, scr)
            slk = wload(1792 + pr * 128, 128)
            qk_chunk(U, slk, 0, [(0, 128, KT)], lambda g: (U + 'KT', g), False, 0, tabs, scr)
            slv = wload(2048 + pr * 256, 256)
            for tt in range(NT):
                b = tt % 2
                pk = 'ps%d' % (6 + b)
                for k in range(NCH):
                    p.op('pe', lambda e, b=b, k=k, tt=tt, slv=slv: e.matmul(ps[6 + b][:, 0:256], lhsT=HT[:, k, tt * 128:(tt + 1) * 128],
                                                                 rhs=WS[slv][:, k, 0:256], start=(k == 0), stop=(k == NCH - 1)),
                         reads=['WS%d' % slv, htk(k, tt // 4)], writes=[pk], mark=(k == NCH - 1))
                if tt % 2 == 0:
                    p.op('act', lambda e, b=b, tt=tt, VR=VR: e.copy(out=VR[:, tt, :], in_=ps[6 + b][:, 0:256]),
                         reads=[pk], writes=[(U + 'VR', tt)])
                else:
                    p.op('dve', lambda e, b=b, tt=tt, VR=VR: e.tensor_copy(out=VR[:, tt, :], in_=ps[6 + b][:, 0:256]),
                         reads=[pk], writes=[(U + 'VR', tt)])
            slg = wload(2560 + pr * 256, 256)
            for tt in range(NT):
                b = tt % 2
                pk = 'ps%d' % (6 + b)
                for k in range(NCH):
                    p.op('pe', lambda e, b=b, k=k, tt=tt, slg=slg: e.matmul(ps[6 + b][:, 0:256], lhsT=HT[:, k, tt * 128:(tt + 1) * 128],
                                                                 rhs=WS[slg][:, k, 0:256], start=(k == 0), stop=(k == NCH - 1)),
                         reads=['WS%d' % slg, htk(k, tt // 4)], writes=[pk], mark=(k == NCH - 1))
                p.op('act', lambda e, b=b, tt=tt, SG=SG: e.activation(out=SG[:, tt, :], in_=ps[6 + b][:, 0:256], func=AF.Silu),
                     reads=[pk], writes=[(U + 'SG', tt)])
            ET = mem.view(U + 'ET', O_SC, F32, [640])
            DG = mem.view(U + 'DG', O_SC + 2560, F32, [640])
            TX = mem.view(U + 'TX', O_SC + 5120, F32, [640])
            TMk = mem.view(U + 'TMk', O_SC + 8192, F32, [2, 640])
            p.dma('sp', [(ET, etab_d), (DG, diag2_d)], writes=[U + 'ET', U + 'DG'], sem='D_ETDG')
            for hl in range(2):
                h = pr * 2 + hl
                p.op('dve', lambda e, hl=hl, h=h, TMk=TMk, ET=ET: e.tensor_scalar(
                    out=TMk[:, hl, :], in0=ET, scalar1=0.0, scalar2=LG[:, h:h + 1], op0=ALU.max, op1=ALU.mult),
                    reads=[U + 'ET', 'LG'], writes=[(U + 'TMk', hl)])
                p.op('dve', lambda e, h=h, TX=TX, ET=ET: e.tensor_scalar(
                    out=TX, in0=ET, scalar1=0.0, scalar2=LGN[:, 4 + h:5 + h], op0=ALU.min, op1=ALU.mult),
                    reads=[U + 'ET', 'LGN'], writes=[U + 'TX'])
                p.op('dve', lambda e, hl=hl, TMk=TMk, TX=TX: e.tensor_tensor(out=TMk[:, hl, :], in0=TMk[:, hl, :], in1=TX, op=ALU.add),
                     reads=[U + 'TX', (U + 'TMk', hl)], writes=[(U + 'TMk', hl)])
                p.op('act', lambda e, hl=hl, TMk=TMk: e.activation(out=TMk[:, hl, :], in_=TMk[:, hl, :], func=AF.Exp),
                     reads=[(U + 'TMk', hl)], writes=[(U + 'TMk', hl)])
                p.op('dve', lambda e, hl=hl, TMk=TMk, DG=DG: e.tensor_tensor(out=TMk[:, hl, :], in0=TMk[:, hl, :], in1=DG, op=ALU.mult),
                     reads=[U + 'DG', (U + 'TMk', hl)], writes=[(U + 'TMk', hl)])
            PT = mem.view(U + 'PT', O_SC + 13312, BF16, [2, 2, 256])
            EO = O_SC + 15360
            RC = mem.view(U + 'RC', EO, F32, [2, 2, 2])
            YN = mem.view(U + 'YN', EO + 64, F32, [4, 128])
            JK = mem.view(U + 'JK', EO + 64 + 2048, F32, [128])
            YG = mem.view(U + 'YG', EO + 64 + 2560, BF16, [4, 128])
            def rt_qk(qg, kt, pr=pr, QT=QT, KT=KT, PT=PT, TMk=TMk, U=U):
                b = kt % 2
                dl = 2 * qg - kt
                for hl in range(2):
                    h = pr * 2 + hl
                    pk = 'ps%d' % (b * 2 + hl)
                    p.op('pe', lambda e, b=b, hl=hl, kt=kt, qg=qg, QT=QT, KT=KT: e.matmul(
                        ps[b * 2 + hl][:, 0:256], lhsT=KT[hl * 64:(hl + 1) * 64, kt * 128:(kt + 1) * 128],
                        rhs=QT[hl * 64:(hl + 1) * 64, qg * 256:(qg + 1) * 256], start=True, stop=True),
                        reads=[(U + 'KT', kt // 4), (U + 'QT', qg // 2)], writes=[pk])
                    if dl >= 1:
                        c0, sc = 384, SCT[:, h, dl - 1:dl]
                    elif dl <= -2:
                        c0, sc = 0, SCT[:, 4 + h, -dl - 2:-dl - 1]
                    else:
                        c0, sc = (256 if dl == 0 else 128), None
                    if sc is not None:
                        p.op('dve', lambda e, b=b, hl=hl, c0=c0, sc=sc, PT=PT, TMk=TMk: e.scalar_tensor_tensor(
                            out=PT[:, b, hl, :], in0=ps[b * 2 + hl][:, 0:256], scalar=sc, in1=TMk[:, hl, c0:c0 + 256],
                            op0=ALU.mult, op1=ALU.mult),
                            reads=[pk, (U + 'TMk', hl)] + sct_keys, writes=[(U + 'PT', b, hl)])
                    else:
                        p.op('dve', lambda e, b=b, hl=hl, c0=c0, PT=PT, TMk=TMk: e.tensor_tensor(
                            out=PT[:, b, hl, :], in0=ps[b * 2 + hl][:, 0:256], in1=TMk[:, hl, c0:c0 + 256], op=ALU.mult),
                            reads=[pk, (U + 'TMk', hl)], writes=[(U + 'PT', b, hl)])

            ACR = mem.view(U + 'ACR', O_SC, F32, [2, 256])

            def rt_pv(qg, kt, pr=pr, PT=PT, VR=VR, RC=RC, YN=YN, JK=JK, YG=YG, SG=SG, U=U, ACR=ACR):
                b = kt % 2
                if kt == 4:
                    flush()
                for qt in range(2):
                    for hl in range(2):
                        first = (kt == 0 and hl == 0)
                        p.op('pe', lambda e, qt=qt, hl=hl, b=b, kt=kt, first=first, PT=PT, VR=VR: e.matmul(
                            ps[4 + qt][:, hl * 128:(hl + 1) * 128], lhsT=PT[:, b, hl, qt * 128:(qt + 1) * 128],
                            rhs=VR[:, kt, hl * 128:(hl + 1) * 128], start=first, stop=(kt == NT - 1), skip_group_check=True),
                            reads=[(U + 'PT', b, hl), (U + 'VR', kt)], writes=['ps%d' % (4 + qt)], mark=(hl == 1))
                if kt != NT - 1:
                    return
                for qt in range(2):
                    if qt == 0:
                        p.op('act', lambda e, qt=qt, ACR=ACR: e.copy(out=ACR[:, qt, :], in_=ps[4 + qt][:, 0:256]),
                             reads=['ps%d' % (4 + qt)], writes=[(U + 'ACR', qt)])
                    else:
                        p.op('dve', lambda e, qt=qt, ACR=ACR: e.tensor_copy(out=ACR[:, qt, :], in_=ps[4 + qt][:, 0:256]),
                             reads=['ps%d' % (4 + qt)], writes=[(U + 'ACR', qt)])
                for qt in range(2):
                    ak = (U + 'ACR', qt)
                    for hl in range(2):
                        Y = ACR[:, qt, hl * 128:(hl + 1) * 128]
                        sk_ = (U + 'RC', qt, hl)
                        p.op('act', lambda e, Y=Y, qt=qt, hl=hl, JK=JK, RC=RC: e.activation(out=JK, in_=Y, func=AF.Square,
                                                                                         accum_out=RC[:, qt, hl, 0:1]),
                             reads=[ak], writes=[U + 'JK', sk_])
                        p.op('act', lambda e, qt=qt, hl=hl, RC=RC: e.activation(out=RC[:, qt, hl, 0:1], in_=RC[:, qt, hl, 0:1],
                                                                              func=AF.Ln, scale=1.0 / 128, bias=EPSC),
                             reads=[sk_, 'EPSC'], writes=[sk_])
                        p.op('act', lambda e, qt=qt, hl=hl, RC=RC: e.activation(out=RC[:, qt, hl, 0:1], in_=RC[:, qt, hl, 0:1],
                                                                              func=AF.Exp, scale=-0.5),
                             reads=[sk_], writes=[sk_])
                for qt in range(2):
                    ak = (U + 'ACR', qt)
                    tq = qg * 2 + qt
                    for hl in range(2):
                        h = pr * 2 + hl
                        Y = ACR[:, qt, hl * 128:(hl + 1) * 128]
                        sk_ = (U + 'RC', qt, hl)
                        yi = qt * 2 + hl
                        p.op('dve', lambda e, Y=Y, qt=qt, hl=hl, yi=yi, RC=RC, YN=YN: e.scalar_tensor_tensor(
                            out=YN[:, yi, :], in0=Y, scalar=RC[:, qt, hl, 0:1], in1=GRET, op0=ALU.mult, op1=ALU.mult),
                            reads=[ak, sk_, 'GRET'], writes=[(U + 'YN', yi)])
                        p.op('dve', lambda e, hl=hl, tq=tq, yi=yi, YN=YN, YG=YG, SG=SG: e.tensor_tensor(
                            out=YG[:, yi, :], in0=YN[:, yi, :], in1=SG[:, tq, hl * 128:(hl + 1) * 128], op=ALU.mult),
                            reads=[(U + 'YN', yi), (U + 'SG', tq)], writes=[(U + 'YG', yi)])
                        defer_transpose(YG[:, yi, :], (U + 'YG', yi), YT[:, h, tq * 128:(tq + 1) * 128], ('YT', h, tq // 4))

            its = [(qg, kt) for qg in range(8) for kt in range(NT)]
            rt_qk(*its[0])
            for i, it in enumerate(its):
                if i + 1 < len(its):
                    rt_qk(*its[i + 1])
                rt_pv(*it)
            flush()

        if STAGE == 3:
            return

        MG = mem.view('MG', O_RT, BF16, [NCH, S])
        MWA = [mem.view('MWA%d' % i, O_W + i * 6144, BF16, [NCH, 128]) for i in range(2)]
        MWR = [mem.view('MWR%d' % i, O_W + i * 6144 + 2048, BF16, [NCH, 128]) for i in range(2)]
        MBA = [mem.view('MBA%d' % i, O_W + i * 6144 + 4096, BF16, [4, 128]) for i in range(2)]
        MBR = [mem.view('MBR%d' % i, O_W + i * 6144 + 5120, BF16, [4, 128]) for i in range(2)]
        SA = [mem.view('SA%d' % i, O_SC + i * 2048, F32, [512]) for i in range(2)]
        SR = [mem.view('SR%d' % i, O_SC + 4096 + i * 2048, F32, [512]) for i in range(2)]
        M1 = [mem.view('M1_%d' % i, O_SC + 8192 + i * 2048, F32, [512]) for i in range(2)]
        M2 = [mem.view('M2_%d' % i, O_SC + 12288 + i * 2048, F32, [512]) for i in range(2)]
        wba_v = wba_d.rearrange("(c p) n -> p c n", p=128)
        wbr_v = wbr_d.rearrange("(c p) n -> p c n", p=128)
        wout_v = wout_d.rearrange("(c p) n -> p c n", p=128)
        it = 0
        for d in range(NCH):
            sl = d % 2
            wk = ['MWA%d' % sl, 'MWR%d' % sl, 'MBA%d' % sl, 'MBR%d' % sl]
            p.dma('pool', [(MWA[sl][:, 0:4, :], win_v[:, 0:4, 3072 + d * 128:3072 + (d + 1) * 128]),
                           (MWA[sl][:, 4:8, :], win_v[:, 4:8, 3072 + d * 128:3072 + (d + 1) * 128]),
                           (MWR[sl][:, 0:4, :], win_v[:, 0:4, 4096 + d * 128:4096 + (d + 1) * 128]),
                           (MWR[sl][:, 4:8, :], win_v[:, 4:8, 4096 + d * 128:4096 + (d + 1) * 128]),
                           (MBA[sl], wba_v[:, :, d * 128:(d + 1) * 128]),
                           (MBR[sl], wbr_v[:, :, d * 128:(d + 1) * 128])], writes=wk, sem='D_MW%d' % sl)
            for tq in range(4):
                b = it % 2; it += 1
                tk = slice(tq * 512, (tq + 1) * 512)
                A, Bp, C, Dp = ps[b * 4], ps[b * 4 + 1], ps[b * 4 + 2], ps[b * 4 + 3]
                ka, kb, kc, kd = ['ps%d' % (b * 4 + i) for i in range(4)]
                for c in range(4):
                    p.op('pe', lambda e, A=A, c=c, tk=tk, sl=sl: e.matmul(A[:, :], lhsT=MBA[sl][:, c, :], rhs=OAT[:, c, tk],
                                                                       start=(c == 0), stop=(c == 3)),
                         reads=[wk[2], ('OAT', c, tq)], writes=[ka], mark=(c == 3))
                for c in range(4):
                    p.op('pe', lambda e, Bp=Bp, c=c, tk=tk, sl=sl: e.matmul(Bp[:, :], lhsT=MBR[sl][:, c, :], rhs=YT[:, c, tk],
                                                                         start=(c == 0), stop=(c == 3)),
                         reads=[wk[3], ('YT', c, tq)], writes=[kb], mark=(c == 3))
                for k in range(NCH):
                    p.op('pe', lambda e, C=C, k=k, tk=tk, sl=sl: e.matmul(C[:, :], lhsT=MWA[sl][:, k, :], rhs=HT[:, k, tk],
                                                                       start=(k == 0), stop=(k == NCH - 1)),
                         reads=[wk[0], htk(k, tq)], writes=[kc], mark=(k == NCH - 1))
                for k in range(NCH):
                    p.op('pe', lambda e, Dp=Dp, k=k, tk=tk, sl=sl: e.matmul(Dp[:, :], lhsT=MWR[sl][:, k, :], rhs=HT[:, k, tk],
                                                                         start=(k == 0), stop=(k == NCH - 1)),
                         reads=[wk[1], htk(k, tq)], writes=[kd], mark=(k == NCH - 1))
                p.op('act', lambda e, C=C, b=b, d=d: e.activation(out=SA[b], in_=C[:, :], func=AF.Sigmoid, bias=BM[:, d:d + 1]),
                     reads=[kc, 'BM'], writes=['SA%d' % b])
                p.op('act', lambda e, Dp=Dp, b=b, d=d: e.activation(out=SR[b], in_=Dp[:, :], func=AF.Sigmoid, bias=BM[:, 8 + d:9 + d]),
                     reads=[kd, 'BM'], writes=['SR%d' % b])
                p.op('dve', lambda e, A=A, b=b: e.tensor_tensor(out=M1[b], in0=A[:, :], in1=SA[b], op=ALU.mult),
                     reads=[ka, 'SA%d' % b], writes=['M1_%d' % b])
                p.op('dve', lambda e, Bp=Bp, b=b: e.tensor_tensor(out=M2[b], in0=Bp[:, :], in1=SR[b], op=ALU.mult),
                     reads=[kb, 'SR%d' % b], writes=['M2_%d' % b])
                p.op('dve', lambda e, b=b, d=d, tk=tk: e.tensor_tensor(out=MG[:, d, tk], in0=M1[b], in1=M2[b], op=ALU.add),
                     reads=['M1_%d' % b, 'M2_%d' % b], writes=[('MG', d, tq)])
        WO = [mem.view('WO%d' % i, O_W + i * 4096, BF16, [NCH, 256]) for i in range(2)]
        it = 0
        for j in range(4):
            sl = j % 2
            p.dma('pool', [(WO[sl][:, 0:4, :], wout_v[:, 0:4, j * 256:(j + 1) * 256]),
                           (WO[sl][:, 4:8, :], wout_v[:, 4:8, j * 256:(j + 1) * 256])], writes=['WO%d' % sl], sem='D_WO%d' % sl)
            for dl in range(2):
                d = j * 2 + dl
                for tq in range(4):
                    pb = it % 2; it += 1
                    pk = 'ps%d' % pb
                    tk = slice(tq * 512, (tq + 1) * 512)
                    for k in range(NCH):
                        p.op('pe', lambda e, pb=pb, k=k, tk=tk, sl=sl, dl=dl: e.matmul(
                            ps[pb][:, :], lhsT=WO[sl][:, k, dl * 128:(dl + 1) * 128], rhs=MG[:, k, tk],
                            start=(k == 0), stop=(k == NCH - 1)),
                            reads=['WO%d' % sl, ('MG', k, tq)], writes=[pk], mark=(k == NCH - 1))
                    xs_ = XT[:, d, tk]
                    p.op('dve', lambda e, pb=pb, d=d, xs_=xs_: e.scalar_tensor_tensor(
                        out=xs_, in0=ps[pb][:, :], scalar=HG[:, 8 + d:9 + d], in1=xs_, op0=ALU.mult, op1=ALU.add),
                        reads=[pk, ('HG', 1), ('XT', d, tq)], writes=[('XT', d, tq)])

    if STAGE >= 2:
        mixer()
    if STAGE >= 5:
        ffn(2, f2w1_d, f2w3_d, f2w2_d)

    OS_OFF = OFF_FREE
    osb = [mem.view('OS%d' % i, OS_OFF + i * 16384, F32, [4, D]) for i in range(2)]
    o_v = out_d.rearrange("(t p) d -> p t d", p=128)
    for g in range(4):
        sl = g % 2
        key = 'OS%d' % sl
        for c in range(NCH):
            pb = (g * NCH + c) % 2
            pk = 'ps%d' % pb
            for t in range(4):
                p.op('pe', lambda e, g=g, t=t, c=c, pb=pb: e.transpose(
                    ps[pb][:, t * 128:(t + 1) * 128], XT[:, c, (g * 4 + t) * 128:(g * 4 + t + 1) * 128], IDF),
                    reads=[('XT', c, g), 'IDF'], writes=[pk], mark=(t == 3))
            src = ps[pb][:, :].rearrange("p (t f) -> p t f", t=4)
            dst = osb[sl][:, :, c * 128:(c + 1) * 128]
            if c % 2 == 0:
                p.op('dve', lambda e, src=src, dst=dst: e.tensor_copy(out=dst, in_=src),
                     reads=[pk], writes=[(key, c)])
            else:
                p.op('act', lambda e, src=src, dst=dst: e.copy(out=dst, in_=src),
                     reads=[pk], writes=[(key, c)])
        p.dma('sp', [(o_v[:, g * 4 + t:g * 4 + t + 1, :], osb[sl][:, t:t + 1, :]) for t in range(4)],
              reads=[(key, c) for c in range(NCH)], sem='D_st%d' % sl)

    p.finish()
    p.emit()
    return nc


_CACHE = {}


def _consts():
    return {
        "ident": np.eye(128, dtype=np.float32),
        "ones": np.ones((128, 128), dtype=np.float32),
        "blk": np.kron(np.eye(2, dtype=np.float32), np.ones((64, 64), dtype=np.float32)),
        "rmat": _rmat(),
        "inv": _inv(),
        "sgn": np.where((np.arange(128) % 64) < 32, -1.0, 1.0).astype(np.float32).reshape(128, 1),
        "etab": _etab(),
        "diag2": np.where(_etab() == 0, 0.25, 0.125).astype(np.float32),
    }


def _rmat():
    r = np.zeros((128, 128), dtype=np.float32)
    for m in range(128):
        src = (m // 64) * 64 + ((m % 64) + 32) % 64
        r[src, m] = 1.0
    return r


def _inv():
    j = (np.arange(128) % 32).astype(np.float32)
    inv0 = (1.0 / (np.float32(10000.0) ** (np.arange(0, 64, 2, dtype=np.float32) / np.float32(64)))).astype(np.float32)
    inv1 = (1.0 / (np.float32(10000.0) ** np.linspace(0.0, 1.0, 32, dtype=np.float32))).astype(np.float32)
    idx = np.arange(128) % 32
    return np.stack([inv0[idx], inv1[idx]], axis=1).astype(np.float32)


def _etab():
    c = np.arange(640, dtype=np.float32)[None, :]
    kk = np.arange(128, dtype=np.float32)[:, None]
    return (c - kk - 256.0).astype(np.float32)


def kernel(**inp):
    if 'nc' not in _CACHE:
        _CACHE['nc'] = build_program()
    nc = _CACHE['nc']
    f32 = np.float32
    A = lambda a: np.ascontiguousarray(a)
    cst = _consts()
    gains = np.concatenate([inp['norm_ffn1'][0].reshape(NCH, 128).T, inp['norm_mix'][0].reshape(NCH, 128).T,
                            inp['norm_ffn2'][0].reshape(NCH, 128).T], axis=1).astype(f32)
    shared = {
        "w_ada": A(inp['w_ada'][0]), "b_adaT": A(inp['b_ada'][0].reshape(72, 128).T),
        "gainsT": A(gains),
        "f1w1": A(inp['ffn1_w1'][0]), "f1w3": A(inp['ffn1_w3'][0]), "f1w2": A(inp['ffn1_w2'][0]),
        "f2w1": A(inp['ffn2_w1'][0]), "f2w3": A(inp['ffn2_w3'][0]), "f2w2": A(inp['ffn2_w2'][0]),
        "w_in": A(inp['w_in'][0]),
    }
    w_in = np.array(inp['w_in'][0], dtype=f32, copy=True)
    perm = np.concatenate([np.arange(0, 64, 2), np.arange(1, 64, 2)])
    for base in (1536, 1792):
        for h in range(4):
            blkc = w_in[:, base + h * 64: base + (h + 1) * 64]
            w_in[:, base + h * 64: base + (h + 1) * 64] = blkc[:, perm]
    shared["w_in"] = A(w_in)
    shared["gqk"] = A(np.stack([np.tile(inp['da_q_gain'][0], 2), np.tile(inp['da_k_gain'][0], 2)], axis=1).astype(f32))
    lqk = np.concatenate([inp['da_lambda_q1'][0], inp['da_lambda_k1'][0], inp['da_lambda_q2'][0], inp['da_lambda_k2'][0]])
    shared["lqk"] = A(np.broadcast_to(lqk[None, :], (128, 256)).astype(f32))
    shared["gsub"] = A(np.broadcast_to(inp['da_subln'][0][None, :], (128, 128)).astype(f32))
    shared["gret"] = A(np.broadcast_to(inp['ret_norm'][0][None, :], (128, 128)).astype(f32))
    dec = np.concatenate([inp['ret_decay_f'][0], inp['ret_decay_b'][0]])
    shared["dec"] = A(np.broadcast_to(dec[None, :], (128, 8)).astype(f32))
    shared["bmT"] = A(inp['b_merge'][0].reshape(16, 128).T.astype(f32))
    shared["w_ba"] = A(inp['w_branch_a'][0]); shared["w_br"] = A(inp['w_branch_r'][0]); shared["w_out"] = A(inp['w_out'][0])
    shared.update(cst)
    in_maps = []
    for b in range(NB):
        m = dict(shared)
        m["x"] = A(inp['x'][b])
        m["cT"] = A(inp['c'][b].reshape(NCH, 128).T)
        m["pos"] = A(np.broadcast_to(inp['positions'][b][None, :], (128, S))).astype(np.int32)
        in_maps.append(m)
    res = run_bass_kernel_spmd(nc, in_maps, core_ids=list(range(NB)))
    _CACHE['last'] = res
    return np.stack([r["out"] for r in res.results], axis=0).astype(np.float32)
```
